# Optimizing a Trainium2 kernel written in Bass

```python
import jax, jax.numpy as jnp
from jax import lax
import numpy as np

D_MODEL = 1024
BATCH = 4
SEQ = 8192
DEPTH = 2

DEEPNORM_ALPHA = (2 * DEPTH) ** 0.25
DEEPNORM_BETA = (8 * DEPTH) ** -0.25
LN_EPS = 1e-5

CONV_CH = D_MODEL
CONV_WIDTH = 31
SSM_INNER = D_MODEL
SSM_HEAD_DIM = 64
SSM_HEADS = SSM_INNER // SSM_HEAD_DIM
SSM_GROUPS = 2
SSM_HPG = SSM_HEADS // SSM_GROUPS
SSM_STATE = 128
SSM_CONV = 4
SSM_CHUNK = 128
SSM_XBC = SSM_INNER + 2 * SSM_GROUPS * SSM_STATE
EVEN_IN = 2 * CONV_CH + SSM_INNER + SSM_XBC + SSM_HEADS
EVEN_MIX = CONV_CH + SSM_INNER

GLA_HEADS = 4
GLA_DK_TOTAL = D_MODEL // 2
GLA_DV_TOTAL = D_MODEL
GLA_DK = GLA_DK_TOTAL // GLA_HEADS
GLA_DV = GLA_DV_TOTAL // GLA_HEADS
GLA_RANK = 16
GLA_TAU = 16.0
GLA_CHUNK = 64
ODD_IN = 2 * GLA_DK_TOTAL + 2 * GLA_DV_TOTAL + GLA_RANK

MEM_LEN = 256
XA_HEADS = 4
XA_HEAD_DIM = D_MODEL // XA_HEADS
D_FF = 4 * D_MODEL

N_EVEN = (DEPTH + 1) // 2
N_ODD = DEPTH // 2

kernel_name = "hybrid_conv_ssd_gla_deepnorm_trunk"


def layer_norm(x, g, b):
    xf = x.astype(jnp.float32)
    mu = jnp.mean(xf, -1, keepdims=True)
    var = jnp.mean(jnp.square(xf - mu), -1, keepdims=True)
    return ((xf - mu) * lax.rsqrt(var + LN_EPS) * g + b).astype(x.dtype)


def rms_norm(x, g):
    xf = x.astype(jnp.float32)
    return (xf * lax.rsqrt(jnp.mean(xf * xf, -1, keepdims=True) + LN_EPS) * g).astype(x.dtype)


def causal_depthwise_conv(x, w, b):
    width = w.shape[0]
    y = lax.conv_general_dilated(x, w[:, None, :].astype(x.dtype), window_strides=(1,),
                                 padding=[(width - 1, 0)],
                                 dimension_numbers=('NWC', 'WIO', 'NWC'),
                                 feature_group_count=x.shape[-1])
    return y + b


def ssd_chunked(x, a, b_mat, c_mat):
    bsz, seqlen = x.shape[:2]
    nc, q = seqlen // SSM_CHUNK, SSM_CHUNK
    xc = x.reshape(bsz, nc, q, SSM_GROUPS, SSM_HPG, SSM_HEAD_DIM)
    ac = jnp.moveaxis(a.astype(jnp.float32).reshape(bsz, nc, q, SSM_GROUPS, SSM_HPG), 2, -1)
    bm = b_mat.reshape(bsz, nc, q, SSM_GROUPS, SSM_STATE)
    cm = c_mat.reshape(bsz, nc, q, SSM_GROUPS, SSM_STATE)
    a_cs = jnp.cumsum(ac, axis=-1)
    causal = jnp.tril(jnp.ones((q, q), dtype=bool))
    seg = a_cs[..., :, None] - a_cs[..., None, :]
    decay = jnp.exp(jnp.where(causal, seg, -jnp.inf)).astype(x.dtype)
    cb = jnp.einsum('bclgn,bcsgn->bcgls', cm, bm)
    y_diag = jnp.einsum('bcgkls,bcsgkp->bclgkp', cb[:, :, :, None] * decay, xc)
    decay_to_end = jnp.exp(a_cs[..., -1:] - a_cs).astype(x.dtype)
    states = jnp.einsum('bcsgn,bcgks,bcsgkp->bcgkpn', bm, decay_to_end, xc)
    chunk_decay = jnp.exp(a_cs[..., -1])

    def step(h, inp):
        st, dec = inp
        return h * dec[..., None, None] + st, h

    h0 = jnp.zeros((bsz, SSM_GROUPS, SSM_HPG, SSM_HEAD_DIM, SSM_STATE), jnp.float32)
    _, prev = lax.scan(step, h0, (jnp.moveaxis(states, 1, 0), jnp.moveaxis(chunk_decay, 1, 0)))
    prev = jnp.moveaxis(prev, 0, 1)
    y_off = jnp.einsum('bclgn,bcgkpn,bcgkl->bclgkp', cm, prev, jnp.exp(a_cs))
    return (y_diag + y_off).astype(x.dtype).reshape(bsz, seqlen, SSM_HEADS, SSM_HEAD_DIM)


def even_mixer(x, w_in, conv_w, conv_b, conv_ln_g, conv_ln_b, ssm_conv_w, ssm_conv_b,
               dt_bias, a_log, d_skip, ssm_norm_g, w_out):
    bsz, seqlen, _ = x.shape
    proj = x @ w_in
    c0 = CONV_CH
    c1 = 2 * CONV_CH
    c2 = c1 + SSM_INNER
    c3 = c2 + SSM_XBC
    conv_val, conv_gate, z, xbc, dt = jnp.split(proj, [c0, c1, c2, c3], axis=-1)
    u = conv_val * jax.nn.sigmoid(conv_gate)
    u = causal_depthwise_conv(u, conv_w, conv_b)
    u = jax.nn.silu(layer_norm(u, conv_ln_g, conv_ln_b))
    xbc = jax.nn.silu(causal_depthwise_conv(xbc, ssm_conv_w, ssm_conv_b))
    xs, bm, cm = jnp.split(xbc, [SSM_INNER, SSM_INNER + SSM_GROUPS * SSM_STATE], axis=-1)
    dt = jax.nn.softplus((dt + dt_bias).astype(jnp.float32))
    a = -jnp.exp(a_log.astype(jnp.float32))
    xs_h = xs.reshape(bsz, seqlen, SSM_HEADS, SSM_HEAD_DIM)
    y = ssd_chunked((xs_h * dt[..., None]).astype(x.dtype), dt * a,
                    bm.reshape(bsz, seqlen, SSM_GROUPS, SSM_STATE),
                    cm.reshape(bsz, seqlen, SSM_GROUPS, SSM_STATE))
    y = (y + xs_h * d_skip[:, None]).reshape(bsz, seqlen, SSM_INNER) * jax.nn.silu(z)
    y = rms_norm(y.reshape(bsz, seqlen, SSM_GROUPS, SSM_INNER // SSM_GROUPS),
                 ssm_norm_g.reshape(SSM_GROUPS, SSM_INNER // SSM_GROUPS)).reshape(bsz, seqlen, SSM_INNER)
    return jnp.concatenate([u, y.astype(u.dtype)], axis=-1) @ w_out


def gla_chunked(q, k, v, log_g):
    bsz, seqlen = q.shape[:2]
    n, c = seqlen // GLA_CHUNK, GLA_CHUNK
    q = q.reshape(bsz, n, c, GLA_HEADS, GLA_DK)
    k = k.reshape(bsz, n, c, GLA_HEADS, GLA_DK)
    v = v.reshape(bsz, n, c, GLA_HEADS, GLA_DV)
    b_cs = jnp.cumsum(log_g.astype(jnp.float32).reshape(bsz, n, c, GLA_HEADS, GLA_DK), axis=2)
    q_t = q * jnp.exp(b_cs) * (GLA_DK ** -0.5)
    k_t = k * jnp.exp(-b_cs)
    k_end = k * jnp.exp(b_cs[:, :, -1:] - b_cs)
    causal = jnp.tril(jnp.ones((c, c), dtype=bool))
    att = jnp.where(causal, jnp.einsum('bnlhd,bnshd->bnhls', q_t, k_t), 0.0)
    o = jnp.einsum('bnhls,bnshv->bnlhv', att, v)
    kv = jnp.einsum('bnshd,bnshv->bnhdv', k_end, v)
    chunk_decay = jnp.exp(b_cs[:, :, -1])

    def step(s, inp):
        kv_c, dec = inp
        return s * dec[..., None] + kv_c, s

    s0 = jnp.zeros((bsz, GLA_HEADS, GLA_DK, GLA_DV), jnp.float32)
    _, prev = lax.scan(step, s0, (jnp.moveaxis(kv, 1, 0), jnp.moveaxis(chunk_decay, 1, 0)))
    prev = jnp.moveaxis(prev, 0, 1)
    o = o + jnp.einsum('bnlhd,bnhdv->bnlhv', q_t, prev)
    return o.astype(v.dtype).reshape(bsz, seqlen, GLA_HEADS, GLA_DV)


def odd_mixer(x, w_in, w_gate2, b_gate, head_norm_g, w_out):
    bsz, seqlen, _ = x.shape
    proj = x @ w_in
    s0 = GLA_DK_TOTAL
    s1 = 2 * GLA_DK_TOTAL
    s2 = s1 + GLA_DV_TOTAL
    s3 = s2 + GLA_DV_TOTAL
    q, k, v, g_out, g_low = jnp.split(proj, [s0, s1, s2, s3], axis=-1)
    log_g = jax.nn.log_sigmoid((g_low @ w_gate2 + b_gate).astype(jnp.float32)) / GLA_TAU
    o = gla_chunked(q.reshape(bsz, seqlen, GLA_HEADS, GLA_DK),
                    k.reshape(bsz, seqlen, GLA_HEADS, GLA_DK),
                    v.reshape(bsz, seqlen, GLA_HEADS, GLA_DV),
                    log_g.reshape(bsz, seqlen, GLA_HEADS, GLA_DK))
    o = rms_norm(o, head_norm_g).reshape(bsz, seqlen, GLA_DV_TOTAL) * jax.nn.silu(g_out)
    return o @ w_out


def memory_cross_attention(x, mem, w_q, w_k, w_v, w_o):
    bsz, seqlen, _ = x.shape
    q = (x @ w_q).reshape(bsz, seqlen, XA_HEADS, XA_HEAD_DIM)
    k = (mem @ w_k).reshape(bsz, mem.shape[1], XA_HEADS, XA_HEAD_DIM)
    v = (mem @ w_v).reshape(bsz, mem.shape[1], XA_HEADS, XA_HEAD_DIM)
    s = jnp.einsum('blhd,bmhd->bhlm', q, k).astype(jnp.float32) * (XA_HEAD_DIM ** -0.5)
    p = jax.nn.softmax(s, axis=-1).astype(v.dtype)
    o = jnp.einsum('bhlm,bmhd->blhd', p, v).reshape(bsz, seqlen, D_MODEL)
    return o @ w_o


def sq_relu_mlp(x, w1, w2):
    return jnp.square(jax.nn.relu(x @ w1)) @ w2


def setup_inputs(seed: int = 0) -> dict:
    key = jax.random.key(seed)
    ks = iter(jax.random.split(key, 40))
    f32 = jnp.float32

    def nrm(shape, scale):
        return jax.random.normal(next(ks), shape, f32) * scale

    dt0 = jnp.exp(jax.random.uniform(next(ks), (N_EVEN, SSM_HEADS), f32,
                                     np.log(1e-3).astype(np.float32), np.log(1e-1).astype(np.float32)))
    return {
        "x": nrm((BATCH, SEQ, D_MODEL), 1.0),
        "mem": nrm((BATCH, MEM_LEN, D_MODEL), 1.0),
        "even_w_in": nrm((N_EVEN, D_MODEL, EVEN_IN), D_MODEL ** -0.5),
        "even_conv_w": nrm((N_EVEN, CONV_WIDTH, CONV_CH), CONV_WIDTH ** -0.5),
        "even_conv_b": nrm((N_EVEN, CONV_CH), 0.01),
        "even_conv_ln_g": 1.0 + nrm((N_EVEN, CONV_CH), 0.02),
        "even_conv_ln_b": nrm((N_EVEN, CONV_CH), 0.02),
        "even_ssm_conv_w": nrm((N_EVEN, SSM_CONV, SSM_XBC), SSM_CONV ** -0.5),
        "even_ssm_conv_b": nrm((N_EVEN, SSM_XBC), 0.01),
        "even_dt_bias": dt0 + jnp.log(-jnp.expm1(-dt0)),
        "even_a_log": jnp.log(jax.random.uniform(next(ks), (N_EVEN, SSM_HEADS), f32, 1.0, 16.0)),
        "even_d_skip": 1.0 + nrm((N_EVEN, SSM_HEADS), 0.02),
        "even_ssm_norm_g": 1.0 + nrm((N_EVEN, SSM_INNER), 0.02),
        "even_w_out": nrm((N_EVEN, EVEN_MIX, D_MODEL), EVEN_MIX ** -0.5 * DEEPNORM_BETA),
        "odd_w_in": nrm((N_ODD, D_MODEL, ODD_IN), D_MODEL ** -0.5),
        "odd_w_gate2": nrm((N_ODD, GLA_RANK, GLA_DK_TOTAL), GLA_RANK ** -0.5),
        "odd_b_gate": nrm((N_ODD, GLA_DK_TOTAL), 0.1),
        "odd_head_norm_g": 1.0 + nrm((N_ODD, GLA_DV), 0.02),
        "odd_w_out": nrm((N_ODD, GLA_DV_TOTAL, D_MODEL), GLA_DV_TOTAL ** -0.5 * DEEPNORM_BETA),
        "xa_w_q": nrm((DEPTH, D_MODEL, D_MODEL), D_MODEL ** -0.5),
        "xa_w_k": nrm((DEPTH, D_MODEL, D_MODEL), D_MODEL ** -0.5),
        "xa_w_v": nrm((DEPTH, D_MODEL, D_MODEL), D_MODEL ** -0.5),
        "xa_w_o": nrm((DEPTH, D_MODEL, D_MODEL), D_MODEL ** -0.5 * DEEPNORM_BETA),
        "mlp_w1": nrm((DEPTH, D_MODEL, D_FF), D_MODEL ** -0.5),
        "mlp_w2": nrm((DEPTH, D_FF, D_MODEL), D_FF ** -0.5 * DEEPNORM_BETA),
        "ln_g": 1.0 + nrm((DEPTH, 3, D_MODEL), 0.02),
        "ln_b": nrm((DEPTH, 3, D_MODEL), 0.02),
    }


def reference(x, mem, even_w_in, even_conv_w, even_conv_b, even_conv_ln_g, even_conv_ln_b,
              even_ssm_conv_w, even_ssm_conv_b, even_dt_bias, even_a_log, even_d_skip,
              even_ssm_norm_g, even_w_out, odd_w_in, odd_w_gate2, odd_b_gate, odd_head_norm_g,
              odd_w_out, xa_w_q, xa_w_k, xa_w_v, xa_w_o, mlp_w1, mlp_w2, ln_g, ln_b):
    h = x
    for i in range(DEPTH):
        j = i // 2
        if i % 2 == 0:
            m = even_mixer(h, even_w_in[j], even_conv_w[j], even_conv_b[j], even_conv_ln_g[j],
                           even_conv_ln_b[j], even_ssm_conv_w[j], even_ssm_conv_b[j],
                           even_dt_bias[j], even_a_log[j], even_d_skip[j], even_ssm_norm_g[j],
                           even_w_out[j])
        else:
            m = odd_mixer(h, odd_w_in[j], odd_w_gate2[j], odd_b_gate[j], odd_head_norm_g[j],
                          odd_w_out[j])
        h = layer_norm(DEEPNORM_ALPHA * h + m, ln_g[i, 0], ln_b[i, 0])
        h = layer_norm(DEEPNORM_ALPHA * h + memory_cross_attention(h, mem, xa_w_q[i], xa_w_k[i],
                                                                   xa_w_v[i], xa_w_o[i]),
                       ln_g[i, 1], ln_b[i, 1])
        h = layer_norm(DEEPNORM_ALPHA * h + sq_relu_mlp(h, mlp_w1[i], mlp_w2[i]),
                       ln_g[i, 2], ln_b[i, 2])
    return h
```

```python
import math
import numpy as np
import concourse.bass as bass
import concourse.mybir as mybir
from concourse.bass_utils import run_bass_kernel_spmd

F32 = mybir.dt.float32
BF16 = mybir.dt.bfloat16
AF = mybir.ActivationFunctionType
ALU = mybir.AluOpType

D = 1024
T = 512
ALPHA = float((2 * 2) ** 0.25)
EPS = 1e-5
NCORES = 8
ENGINES = ("pe", "act", "dve", "pool", "sp")
GR = 128


class Buf:
    def __init__(self, name, t, nparts=1):
        self.name = name
        self.t = t
        self.nparts = nparts
        self.last_w = [None] * nparts
        self.readers = [[] for _ in range(nparts)]
        self.dma_sem = None
        self.ndma = 0
        self.sem_inc = 16

    def a(self, p=None):
        if p is None:
            return (self, range(self.nparts))
        return (self, (p,))


class Op:
    __slots__ = ("eng", "fn", "deps", "is_dma", "sig_sem", "sig_val", "needs_sig", "idx")


class View:
    def __init__(self, buf, off, words, ap, shape, bpe):
        self.buf, self.off, self.words, self.ap, self.shape, self.bpe = buf, off, words, ap, shape, bpe
        self.strides = []
        s = 1
        for d in reversed(shape):
            self.strides.insert(0, s)
            s *= d

    def __getitem__(self, idx):
        return self.ap[idx]

    def a(self, *idx):
        e0 = 0
        n = self.strides[0] * self.shape[0]
        for k, i in enumerate(idx):
            e0 += i * self.strides[k]
            n = self.strides[k]
        w0 = self.off + (e0 * self.bpe) // 4
        w1 = self.off + (((e0 + n) * self.bpe + 3) // 4)
        return (self.buf, range(w0 // GR, (w1 - 1) // GR + 1))


class Arena:
    def __init__(self, S, name, nwords):
        nwords = ((nwords + GR - 1) // GR) * GR
        self.buf = S.sbuf(name, [128, nwords], F32, nparts=nwords // GR)
        self.nwords = nwords
        self.top = 0

    def alloc(self, shape, dtype, at=None):
        n = 1
        for d in shape:
            n *= d
        bpe = 4 if dtype == F32 else 2
        words = (n * bpe + 3) // 4
        if at is None:
            off = self.top
            self.top = ((off + words + GR - 1) // GR) * GR
        else:
            off = at
        assert off + words <= self.nwords, (self.buf.name, off, words, self.nwords)
        ap = self.buf.t[:, off:off + words]
        if dtype != F32:
            ap = ap.bitcast(dtype)
        if len(shape) == 2:
            ap = ap.rearrange("p (a b) -> p a b", a=shape[0])
        elif len(shape) == 3:
            ap = ap.rearrange("p (a b c) -> p a b c", a=shape[0], b=shape[1])
        return View(self.buf, off, words, ap, list(shape), bpe)


class Sched:
    def __init__(self, nc, same_engine_sync=True):
        self.nc = nc
        self.ops = []
        self.same_engine_sync = same_engine_sync
        self.eng_count = {e: 0 for e in ENGINES}
        self.dma_bufs = []
        self.ctx = []
        self.banks = []
        self.bank_i = 0

    def sbuf(self, name, shape, dtype, nparts=1):
        cm = self.nc.sbuf_tensor(name, list(shape), dtype)
        t = cm.__enter__()
        self.ctx.append(cm)
        return Buf(name, t, nparts)

    def psum(self, name, shape, dtype, nparts=1):
        cm = self.nc.psum_tensor(name, list(shape), dtype)
        t = cm.__enter__()
        self.ctx.append(cm)
        return Buf(name, t, nparts)

    def dram(self, name, shape, dtype, kind="Internal"):
        t = self.nc.dram_tensor(name, list(shape), dtype, kind=kind)
        return Buf(name, t.ap(), 1)

    def ps(self):
        b = self.banks[self.bank_i % len(self.banks)]
        self.bank_i += 1
        return b

    def op(self, eng, fn, reads=(), writes=(), dma_buf=None):
        o = Op()
        o.eng, o.fn, o.idx = eng, fn, len(self.ops)
        o.is_dma = dma_buf is not None
        o.needs_sig = False
        deps = set()
        for b, parts in reads:
            for p in parts:
                if b.last_w[p] is not None:
                    deps.add(b.last_w[p])
        for b, parts in writes:
            for p in parts:
                if b.last_w[p] is not None:
                    deps.add(b.last_w[p])
                deps.update(b.readers[p])
        for b, parts in reads:
            for p in parts:
                b.readers[p].append(o.idx)
        for b, parts in writes:
            for p in parts:
                b.last_w[p] = o.idx
                b.readers[p] = []
        waits = {}
        for d in deps:
            do = self.ops[d]
            if do.is_dma:
                key = ("dma", do.sig_sem)
                val = do.sig_sem.ndma * do.sig_sem.sem_inc
            else:
                if do.eng == eng and not o.is_dma and (eng == "pe" or not self.same_engine_sync):
                    continue
                key = ("eng", do.eng)
                val = do.sig_val
            do.needs_sig = True
            if waits.get(key, 0) < val:
                waits[key] = val
        o.deps = waits
        if o.is_dma:
            if dma_buf.dma_sem is None:
                dma_buf.dma_sem = True
                self.dma_bufs.append(dma_buf)
            dma_buf.ndma += 1
            o.sig_sem = dma_buf
            o.sig_val = dma_buf.ndma * dma_buf.sem_inc
        else:
            self.eng_count[eng] += 1
            o.sig_sem = None
            o.sig_val = self.eng_count[eng]
        self.ops.append(o)
        return o

    def emit(self, final_wait_bufs=()):
        nc = self.nc
        remap = {e: {} for e in ENGINES}
        cnt = {e: 0 for e in ENGINES}
        for o in self.ops:
            if not o.is_dma and o.needs_sig:
                cnt[o.eng] += 1
                remap[o.eng][o.sig_val] = cnt[o.eng]
        sems = {}
        for e in ENGINES:
            cm = nc.semaphore("sem_" + e)
            sems[("eng", e)] = cm.__enter__()
            self.ctx.append(cm)
        for b in self.dma_bufs:
            cm = nc.semaphore("dsem_" + b.name)
            sems[("dma", b)] = cm.__enter__()
            self.ctx.append(cm)
        by_eng = {e: [o for o in self.ops if o.eng == e] for e in ENGINES}
        engobj = {"pe": "tensor", "act": "scalar", "dve": "vector", "pool": "gpsimd", "sp": "sync"}
        stats = {}
        with nc.Block() as block:
            for e in ENGINES:
                ops = by_eng[e]

                def body(eng, ops=ops, e=e):
                    waited = {}
                    nw = 0
                    for o in ops:
                        for key, val in o.deps.items():
                            if key[0] == "eng":
                                val = remap[key[1]][val]
                            if waited.get(key, 0) >= val:
                                continue
                            waited[key] = val
                            eng.wait_ge(sems[key], val)
                            nw += 1
                        ins = o.fn(eng)
                        if o.is_dma:
                            if o.sig_sem.sem_inc == 16:
                                ins.then_inc(sems[("dma", o.sig_sem)], 16)
                            else:
                                ins.then_inc(sems[("dma", o.sig_sem)])
                        elif o.needs_sig:
                            ins.then_inc(sems[("eng", e)], 1)
                    if e == "sp":
                        for b in final_wait_bufs:
                            eng.wait_ge(sems[("dma", b)], b.ndma * b.sem_inc)
                    stats[e] = (len(ops), nw)

                getattr(block, engobj[e])(body)
        self.stats = stats
        return stats


def tile_lhsT(W):
    K, N = W.shape
    kc, nj = K // 128, N // 128
    return np.ascontiguousarray(W.reshape(kc, 128, nj, 128).transpose(2, 1, 0, 3).reshape(nj, 128, kc * 128))


def tile_rhs(W, ncol=512):
    K, N = W.shape
    kc, ng = K // 128, N // ncol
    return np.ascontiguousarray(W.reshape(kc, 128, ng, ncol).transpose(2, 1, 0, 3).reshape(ng, 128, kc * ncol))


def chan(v):
    return np.ascontiguousarray(v.reshape(-1, 128).T)


def rows(v):
    return np.ascontiguousarray(np.broadcast_to(v.reshape(1, -1), (128, v.size)))


class Prog:
    def __init__(self, layers, NT, NP, same_engine_sync=True, stages=("mix", "xa", "mlp")):
        self.layers, self.NT, self.NP = layers, NT, NP
        self.stages = stages
        self.debug = DEBUG
        self.dbg_out = {}
        self.nc = bass.Bass("TRN2", target_bir_lowering=False)
        self.S = Sched(self.nc, same_engine_sync)
        self.inputs = {}
        self.build()

    def din(self, name, shape, dtype=F32):
        b = self.S.dram(name, shape, dtype, kind="ExternalInput")
        self.inputs[name] = b
        return b

    def mm(self, out_ap, out_acc, lhsT, lacc, rhs, racc, start=True, stop=True):
        self.S.op("pe", lambda e: e.matmul(out_ap, lhsT=lhsT, rhs=rhs, start=start, stop=stop),
                  reads=[lacc, racc], writes=[out_acc])

    def tr(self, out_ap, out_acc, in_ap, in_acc, ident):
        self.S.op("pe", lambda e: e.transpose(out_ap, in_ap, ident[0]), reads=[in_acc, ident[1]], writes=[out_acc])

    def act(self, out, oacc, in_, iacc, func, bias=None, scale=None, extra_reads=()):
        kw = {}
        if bias is not None:
            kw["bias"] = bias
        if scale is not None:
            kw["scale"] = scale
        self.S.op("act", lambda e: e.activation(out=out, in_=in_, func=func, **kw),
                  reads=[iacc] + list(extra_reads), writes=[oacc])

    def tt(self, out, oacc, in0, a0, in1, a1, op, eng="dve"):
        self.S.op(eng, lambda e: e.tensor_tensor(out=out, in0=in0, in1=in1, op=op), reads=[a0, a1], writes=[oacc])

    def ts(self, out, oacc, in0, a0, s1, s2, op0, op1=None, extra_reads=(), eng="dve"):
        if op1 is None:
            self.S.op(eng, lambda e: e.tensor_scalar(out=out, in0=in0, scalar1=s1, scalar2=None, op0=op0),
                      reads=[a0] + list(extra_reads), writes=[oacc])
        else:
            self.S.op(eng, lambda e: e.tensor_scalar(out=out, in0=in0, scalar1=s1, scalar2=s2, op0=op0, op1=op1),
                      reads=[a0] + list(extra_reads), writes=[oacc])

    def stt(self, out, oacc, in0, a0, scalar, in1, a1, op0, op1, extra_reads=()):
        self.S.op("dve", lambda e: e.scalar_tensor_tensor(out=out, in0=in0, scalar=scalar, in1=in1, op0=op0, op1=op1),
                  reads=[a0, a1] + list(extra_reads), writes=[oacc])

    def cp(self, out, oacc, in_, iacc, eng="act"):
        if eng == "act":
            self.S.op("act", lambda e: e.copy(out=out, in_=in_), reads=[iacc], writes=[oacc])
        else:
            self.S.op(eng, lambda e: e.tensor_copy(out=out, in_=in_), reads=[iacc], writes=[oacc])

    def dbg(self, name, ap, acc, shape):
        if not getattr(self, "debug", False) or name in self.dbg_out:
            return
        d = self.S.dram("dbg_" + name, [128] + list(shape), F32, kind="ExternalOutput")
        stg = self.S.sbuf("dbgs_" + name, [128] + list(shape), F32)
        self.S.op("dve", lambda e: e.tensor_copy(out=stg.t[:], in_=ap), reads=[acc], writes=[stg.a()])
        self.S.op("sp", lambda e: e.dma_start(out=d.t, in_=stg.t[:]), reads=[stg.a()], writes=[d.a()], dma_buf=stg)
        self.dbg_out[name] = (d, stg)

    def wload(self, src_ap, src_buf, shape):
        sl = self.slots[self.slot_i % len(self.slots)]
        self.slot_i += 1
        n = 1
        for d in shape:
            n *= d
        flat = sl.t[:, 0:n]
        if len(shape) == 3:
            dma_dst = flat.rearrange("p (a b) -> p a b", a=shape[0])
            view = flat.rearrange("p (a b c) -> p a b c", a=shape[0], b=shape[1])
        else:
            dma_dst = flat
            view = flat.rearrange("p (a b) -> p a b", a=shape[0])
        self.S.op("pool", lambda e: e.dma_start(out=dma_dst, in_=src_ap), reads=[src_buf.a()], writes=[sl.a()], dma_buf=sl)
        return view, sl.a()

    def build(self):
        S, nc = self.S, self.nc
        NT, NP = self.NT, self.NP
        NTOK, NPREV = NT * T, max(NP, 1) * T
        L0, L1 = 0 in self.layers, 1 in self.layers
        self.xT = self.din("xT", [D, NTOK])
        self.xpT = self.din("xpT", [D, NPREV])
        self.memT_d = self.din("memT", [D, 256])
        self.flag_d = self.din("flag", [128, 1])
        self.outT = S.dram("outT", [D, NTOK], F32, kind="ExternalOutput")
        NPRM0, NPRM1 = 1460, 320
        W = {}
        if L0:
            W["e_val"] = self.din("e_val", [8, 128, 1024])
            W["e_gate"] = self.din("e_gate", [8, 128, 1024])
            W["e_xbc"] = self.din("e_xbc", [12, 128, 1024])
            W["e_z"] = self.din("e_z", [2, 128, 4096])
            W["e_dt"] = self.din("e_dt", [128, 128])
            W["e_out"] = self.din("e_out", [8, 128, 2048])
            W["prm0"] = self.din("prm0", [128, NPRM0])
        if L1:
            W["o_qk"] = self.din("o_qk", [8, 128, 1024])
            W["o_vg"] = self.din("o_vg", [4, 128, 4096])
            W["o_gl"] = self.din("o_gl", [128, 128])
            W["o_g2"] = self.din("o_g2", [16, 512])
            W["o_out"] = self.din("o_out", [8, 128, 1024])
            W["prm1"] = self.din("prm1", [128, NPRM1])
        for L in self.layers:
            W[f"xq{L}"] = self.din(f"xq{L}", [8, 128, 1024])
            W[f"xk{L}"] = self.din(f"xk{L}", [8, 128, 1024])
            W[f"xv{L}"] = self.din(f"xv{L}", [2, 128, 4096])
            W[f"xo{L}"] = self.din(f"xo{L}", [8, 128, 1024])
            W[f"w1{L}"] = self.din(f"w1{L}", [32, 128, 1024])
            W[f"w2{L}"] = self.din(f"w2{L}", [8, 128, 4096])
        self.W = W
        self.banks = [S.psum(f"bank{i}", [128, 512], F32) for i in range(8)]
        S.banks = self.banks[0:6]
        self.PL = self.banks[6:8]
        self.slots = [S.sbuf(f"slot{i}", [128, 4096], BF16) for i in range(2 if self.debug else 4)]
        self.slot_i = 0
        self.H32 = S.sbuf("H32", [128, 8, 512], F32, nparts=8)
        P = Arena(S, "P", 17536)
        Tm = Arena(S, "Tm", 18688)
        self.P, self.Tm = P, Tm
        self.HT = P.alloc([8, 512], BF16)
        self.identF = P.alloc([128], F32)
        self.identB = P.alloc([128], BF16)
        self.tri = P.alloc([128], F32)
        self.onesF = P.alloc([128], F32)
        self.onesB = P.alloc([128], BF16)
        self.negmask = P.alloc([8, 128], BF16)
        self.flag = P.alloc([1], F32)
        self.EPSC = P.alloc([1], F32)
        kt, vv = P.alloc([8, 256], BF16), P.alloc([2, 1024], BF16)
        self.KT = {L: kt for L in self.layers}
        self.VV = {L: vv for L in self.layers}
        self.HT2 = P.alloc([8, 512], BF16)
        self.HTs = [self.HT, self.HT2]
        self.ch_ht = [Buf("ch_ht0", None), Buf("ch_ht1", None)]
        self.ch_gs = Buf("ch_gs", None)
        self.RB = [P.alloc([512], BF16) for _ in range(2)]
        self.RSQ = [P.alloc([512], BF16) for _ in range(2)]
        self.M32 = P.alloc([512], F32)
        self.RSTD = P.alloc([512], F32)
        self.NMR = P.alloc([512], F32)
        self.ln_i = 0
        if L0:
            self.prm0 = P.alloc([NPRM0], F32)
            self.wdt = P.alloc([8, 16], BF16)
            self.HS32 = P.alloc([1024], F32)
            self.HBF = P.alloc([1024], BF16)
            self.UH = P.alloc([8, 30], BF16)
            self.XRH = P.alloc([12, 3], BF16)
            G = Arena(S, "G", 4608)
            self.DG = [G.alloc([31, 128], BF16) for _ in range(2)]
            self.DGX = [G.alloc([4, 128], BF16) for _ in range(2)]
            self.AROW = P.alloc([16], F32)
            self.DTV = P.alloc([4, 16], F32)
            self.ADT = P.alloc([4, 16], F32)
            self.SM = [P.alloc([8, 16], F32) for _ in range(2)]
            self.SS = P.alloc([4], F32)
        if L1:
            self.prm1 = P.alloc([NPRM1], F32)
            self.wgl = P.alloc([8, 16], BF16)
            self.wg2 = P.alloc([512], BF16)
            self.GS32 = P.alloc([1024], F32)
            self.GSBF = P.alloc([1024], BF16)
            self.rmask = P.alloc([512], F32)
            self.SS1 = P.alloc([8], F32)
        self.setup()
        self.kv_setup(self.layers[0])
        fused = len(self.layers) == 2
        seq = [("pre", self.xpT, t, None) for t in range(NP)] + [("main0", self.xT, t, None) for t in range(NT)]
        hmid = None
        if fused:
            hmid = S.dram("hmid", [D, NTOK], F32)
            hmid.nparts = NT
            hmid.last_w = [None] * NT
            hmid.readers = [[] for _ in range(NT)]
            seq += [("main1", hmid, t, t) for t in range(NT)]
        self.prefetch_ht(seq[0][1], seq[0][2], 0, seq[0][3])
        for i, (kind, src, t, part) in enumerate(seq):
            self.HT = self.HTs[i % 2]
            nxt = seq[i + 1] if i + 1 < len(seq) else None

            def pf():
                if nxt is not None and not (nxt[0] == "main1" and kind != "main1"):
                    self.prefetch_ht(nxt[1], nxt[2], (i + 1) % 2, nxt[3])

            L_first = self.layers[0]
            if kind == "pre":
                if L_first == 0:
                    self.mixer0(state_only=True, halo=(t == NP - 1))
                else:
                    self.mixer1(state_only=True)
                pf()
                if t == NP - 1:
                    self.apply_flag(L_first)
            elif kind == "main0":
                self.load_h32(src, t)
                self.layer(L_first, after_mixer=pf)
                if fused:
                    self.mixer1(state_only=True)
                    self.store(t, hmid, part=t)
                    if t == NT - 1:
                        cc_in = Buf("cc_in", nc.dram_tensor("cc_in", [128, 1024], F32).ap(), 1)
                        cc_out = Buf("cc_out", nc.dram_tensor("cc_out", [256, 1024], F32).ap(), 1)
                        cc_out.sem_inc = 1
                        GS = self.GS32
                        S.op("sp", lambda e: e.dma_start(out=cc_in.t, in_=GS.ap), reads=[GS.a()], writes=[cc_in.a()], dma_buf=cc_in)
                        S.op("pool", lambda e: e.collective_compute("AllGather", ALU.bypass, replica_groups=[[0, 1], [2, 3], [4, 5], [6, 7]],
                                                                    ins=[cc_in.t.opt()], outs=[cc_out.t.opt()]),
                             reads=[cc_in.a()], writes=[cc_out.a()], dma_buf=cc_out)
                        S.op("sp", lambda e: e.dma_start(out=GS.ap, in_=cc_out.t[0:128, :]), reads=[cc_out.a()], writes=[GS.a()],
                             dma_buf=self.ch_gs)
                        self.apply_flag(1)
                        self.kv_setup(1)
                        self.prefetch_ht(hmid, 0, (i + 1) % 2, 0)
                else:
                    self.store(t, self.outT)
            else:
                self.load_h32(src, t, part=part)
                self.layer(1, after_mixer=pf)
                self.store(t, self.outT)
        S.emit(final_wait_bufs=[self.H32] + [v[1] for v in self.dbg_out.values()])

    def setup(self):
        S = self.S
        P = self.P
        Tm_buf = self.Tm.buf
        S.op("pool", lambda e: e.memset(self.EPSC.ap, EPS), writes=[self.EPSC.a()])

        def aff(view, pattern, cm, op, fill, inval):
            S.op("pool", lambda e: e.memset(view.ap, inval), writes=[view.a()])
            S.op("pool", lambda e: e.affine_select(out=view.ap, in_=view.ap, pattern=pattern, compare_op=op,
                                                   fill=fill, base=0, channel_multiplier=cm),
                 reads=[view.a()], writes=[view.a()])

        aff(self.identF, [[-1, 128]], 1, ALU.not_equal, 1.0, 0.0)
        aff(self.identB, [[-1, 128]], 1, ALU.not_equal, 1.0, 0.0)
        aff(self.tri, [[1, 128]], -1, ALU.is_ge, 0.0, 1.0)
        aff(self.negmask, [[0, 8], [1, 128]], -1, ALU.is_ge, -30000.0, 0.0)
        S.op("pool", lambda e: e.memset(self.onesF.ap, 1.0), writes=[self.onesF.a()])
        S.op("pool", lambda e: e.memset(self.onesB.ap, 1.0), writes=[self.onesB.a()])
        S.op("sp", lambda e: e.dma_start(out=self.flag.ap, in_=self.flag_d.t), writes=[self.flag.a()], dma_buf=P.buf)
        if 0 in self.layers:
            S.op("sp", lambda e: e.dma_start(out=self.prm0.ap, in_=self.W["prm0"].t), writes=[self.prm0.a()], dma_buf=P.buf)
            S.op("pool", lambda e: e.dma_start(out=self.wdt.ap, in_=self.W["e_dt"].t.rearrange("p (k c) -> p k c", k=8)),
                 writes=[self.wdt.a()], dma_buf=P.buf)
            for v in (self.HS32, self.HBF, self.UH, self.XRH):
                S.op("pool", lambda e, v=v: e.memset(v.ap, 0.0), writes=[v.a()])
        if 1 in self.layers:
            S.op("sp", lambda e: e.dma_start(out=self.prm1.ap, in_=self.W["prm1"].t), writes=[self.prm1.a()], dma_buf=P.buf)
            S.op("pool", lambda e: e.dma_start(out=self.wgl.ap, in_=self.W["o_gl"].t.rearrange("p (k c) -> p k c", k=8)),
                 writes=[self.wgl.a()], dma_buf=P.buf)
            S.op("pool", lambda e: e.dma_start(out=self.wg2[0:16, :], in_=self.W["o_g2"].t), writes=[self.wg2.a()], dma_buf=P.buf)
            for v in (self.GS32, self.GSBF):
                S.op("pool", lambda e, v=v: e.memset(v.ap, 0.0), writes=[v.a()])
            S.op("pool", lambda e: e.memset(self.rmask.ap, 1.0), writes=[self.rmask.a()])
            S.op("pool", lambda e: e.memset(self.rmask[:, 0:512:128], 0.0), writes=[self.rmask.a()])
        if 0 in self.layers:
            p0 = self.prm0
            self.act(self.AROW.ap, self.AROW.a(), p0[:, self.o_alog:self.o_alog + 16], p0.a(), AF.Exp)
            self.ts(self.AROW.ap, self.AROW.a(), self.AROW.ap, self.AROW.a(), -1.0, None, ALU.mult)

    o_convw, o_convb, o_clng, o_clnb = 0, 248, 256, 264
    o_scw, o_scb, o_dtb, o_alog, o_dsk = 272, 320, 332, 348, 364
    o_normg, o_lng0, o_lnb0 = 380, 1404, 1428
    o_bg, o_hng, o_lng1, o_lnb1 = 0, 4, 260, 284

    def lnp(self, L, i):
        if L == 0:
            p, og, ob = self.prm0, self.o_lng0, self.o_lnb0
        else:
            p, og, ob = self.prm1, self.o_lng1, self.o_lnb1
        return p, og + i * 8, ob + i * 8

    def kv_setup(self, L):
        S = self.S
        self.Tm.top = 0
        self.MEMT = self.Tm.alloc([8, 256], BF16)
        KT, VV, M = self.KT[L], self.VV[L], self.MEMT
        S.op("pool", lambda e: e.dma_start(out=M.ap, in_=self.memT_d.t.rearrange("(j p) m -> p j m", p=128)),
             reads=[self.memT_d.a()], writes=[M.a()], dma_buf=self.Tm.buf)
        wk = self.W[f"xk{L}"]
        for g in range(2):
            w, wacc = self.wload(wk.t[g * 4:(g + 1) * 4].rearrange("g p f -> p g f"), wk, [4, 8, 128])
            for jj in range(4):
                j = g * 4 + jj
                ps = S.ps()
                for kc in range(8):
                    self.mm(ps.t[:, 0:256], ps.a(), w[:, jj, kc, :], wacc, M[:, kc, :], M.a(), kc == 0, kc == 7)
                self.cp(KT[:, j, :], KT.a(j), ps.t[:, 0:256], ps.a())
        wv = self.W[f"xv{L}"]
        for half in range(2):
            w, wacc = self.wload(wv.t[half], wv, [8, 512])
            for mc in range(2):
                ps = S.ps()
                for kc in range(8):
                    self.mm(ps.t[:, :], ps.a(), M[:, kc, mc * 128:(mc + 1) * 128], M.a(), w[:, kc, :], wacc, kc == 0, kc == 7)
                self.cp(VV[:, mc, half * 512:(half + 1) * 512], VV.a(mc), ps.t[:, :], ps.a())

    def layer(self, L, after_mixer=None):
        if "mix" in self.stages:
            if L == 0:
                self.mixer0()
            else:
                self.mixer1()
        if after_mixer is not None:
            after_mixer()
        if "xa" in self.stages:
            self.xattn(L)
        if "mlp" in self.stages:
            self.mlp(L)

    def prefetch_ht(self, src, t, par, part=None):
        HT = self.HTs[par]
        self.S.op("pool", lambda e: e.dma_start(out=HT.ap, in_=src.t[:, t * T:(t + 1) * T].rearrange("(j p) n -> p j n", p=128)),
                  reads=[src.a(part)], writes=[HT.a()], dma_buf=self.ch_ht[par])

    def load_h32(self, src, t, part=None):
        S, H32 = self.S, self.H32
        for j in range(8):
            S.op("sp", lambda e, j=j: e.dma_start(out=H32.t[:, j, :], in_=src.t[j * 128:(j + 1) * 128, t * T:(t + 1) * T]),
                 reads=[src.a(part)], writes=[H32.a(j)], dma_buf=H32)

    def store(self, t, dst, part=None):
        S, H32 = self.S, self.H32
        for j in range(8):
            S.op("sp", lambda e, j=j: e.dma_start(out=dst.t[j * 128:(j + 1) * 128, t * T:(t + 1) * T], in_=H32.t[:, j, :]),
                 reads=[H32.a(j)], writes=[dst.a(part)], dma_buf=H32)

    def apply_flag(self, L):
        st, sb = (self.HS32, self.HBF) if L == 0 else (self.GS32, self.GSBF)
        self.ts(st.ap, st.a(), st.ap, st.a(), self.flag[:, 0:1], None, ALU.mult, extra_reads=[self.flag.a()])
        self.cp(sb.ap, sb.a(), st.ap, st.a())

    def proj_fm(self, w, wacc, jj, KC, rhs_fn, ps):
        for kc in range(KC):
            r, racc = rhs_fn(kc)
            self.mm(ps.t[:, :], ps.a(), w[:, jj, kc, :], wacc, r, racc, kc == 0, kc == KC - 1)

    def ln_stats(self, chunk_fn, n):
        S = self.S
        p0, p1 = self.PL
        for j in range(n):
            r, racc = chunk_fn(j)
            rb, rs = self.RB[self.ln_i % 2], self.RSQ[self.ln_i % 2]
            self.ln_i += 1
            self.cp(rb.ap, rb.a(), r, racc)
            self.act(rs.ap, rs.a(), r, racc, AF.Square)
            self.mm(p0.t[:, :], p0.a(), self.onesB.ap, self.onesB.a(), rb.ap, rb.a(), j == 0, j == n - 1)
            self.mm(p1.t[:, :], p1.a(), self.onesB.ap, self.onesB.a(), rs.ap, rs.a(), j == 0, j == n - 1)
        M, R, N = self.M32, self.RSTD, self.NMR
        inv = 1.0 / (n * 128)
        S.op("act", lambda e: e.mul(M.ap, p0.t[:, :], inv), reads=[p0.a()], writes=[M.a()])
        self.act(N.ap, N.a(), p0.t[:, :], p0.a(), AF.Square, scale=inv)
        self.stt(R.ap, R.a(), p1.t[:, :], p1.a(), inv, N.ap, N.a(), ALU.mult, ALU.subtract)
        self.act(R.ap, R.a(), R.ap, R.a(), AF.Ln, bias=self.EPSC[:, 0:1], extra_reads=[self.EPSC.a()])
        self.act(R.ap, R.a(), R.ap, R.a(), AF.Exp, scale=-0.5)
        self.stt(N.ap, N.a(), M.ap, M.a(), -1.0, R.ap, R.a(), ALU.mult, ALU.mult)

    def ln_resid(self, L, i):
        H32, HT = self.H32, self.HT
        p, og, ob = self.lnp(L, i)
        self.ln_stats(lambda j: (H32.t[:, j, :], H32.a(j)), 8)
        for j in range(8):
            h = H32.t[:, j, :]
            eng = "dve"
            self.tt(h, H32.a(j), h, H32.a(j), self.RSTD.ap, self.RSTD.a(), ALU.mult, eng=eng)
            self.tt(h, H32.a(j), h, H32.a(j), self.NMR.ap, self.NMR.a(), ALU.add, eng=eng)
            self.act(HT[:, j, :], HT.a(j), h, H32.a(j), AF.Identity, bias=p[:, ob + j:ob + j + 1], scale=p[:, og + j:og + j + 1],
                     extra_reads=[p.a()])
        for j in range(8):
            h = H32.t[:, j, :]
            self.act(h, H32.a(j), h, H32.a(j), AF.Identity, bias=p[:, ob + j:ob + j + 1], scale=p[:, og + j:og + j + 1],
                     extra_reads=[p.a()])

    def out_proj_resid(self, wname, KC, rhs_fn, per_slot):
        S, H32 = self.S, self.H32
        wd = self.W[wname]
        w = wacc = None
        for jo in range(8):
            if jo % per_slot == 0:
                w, wacc = self.wload(wd.t[jo:jo + per_slot].rearrange("g p f -> p g f"), wd, [per_slot, KC, 128])
            ps = S.ps()
            self.proj_fm(w, wacc, jo % per_slot, KC, rhs_fn, ps)
            h = H32.t[:, jo, :]
            self.stt(h, H32.a(jo), h, H32.a(jo), ALPHA, ps.t[:, :], ps.a(), ALU.mult, ALU.add)

    def mixer0(self, state_only=False, halo=True):
        S, Tm, HT = self.S, self.Tm, self.HT
        p0 = self.prm0
        Tm.top = 0
        YC = Tm.alloc([8, 512], F32)
        XS = Tm.alloc([4, 1024], F32, at=YC.off)
        UF = Tm.alloc([8, 512], BF16)
        BT = Tm.alloc([2, 512], BF16)
        CT = Tm.alloc([2, 512], BF16)
        BTOK = Tm.alloc([4, 2, 128], BF16)
        SZ = Tm.alloc([4, 1024], BF16)
        YT = Tm.alloc([8, 512], BF16)
        tmp0 = Tm.top
        SG = [Tm.alloc([512], F32) for _ in range(2)]
        XST = [Tm.alloc([512], F32) for _ in range(2)]
        UB = [Tm.alloc([544], BF16) for _ in range(2)]
        XRB = [Tm.alloc([516], BF16) for _ in range(2)]
        Tm.top = tmp0
        DIAG = Tm.alloc([8, 128], F32)
        LEXP = Tm.alloc([8, 128], F32)
        MT = Tm.alloc([8, 128], BF16)
        XDT = Tm.alloc([16, 64], BF16)
        XWs = [Tm.alloc([16, 64], BF16) for _ in range(2)]
        Y32 = Tm.alloc([1024], F32)
        TMP = Tm.alloc([1024], F32)
        YN = Tm.alloc([1024], BF16)
        hrhs = lambda kc: (HT[:, kc, :], HT.a(kc))
        UH, XRH = self.UH, self.XRH

        do_conv = (not state_only) or halo
        if do_conv:
            wts = {}

            def conv_front(j):
                if j % 4 == 0:
                    wts["v"] = self.wload(self.W["e_val"].t[j:j + 4].rearrange("g p f -> p g f"), self.W["e_val"], [4, 8, 128])
                    wts["g"] = self.wload(self.W["e_gate"].t[j:j + 4].rearrange("g p f -> p g f"), self.W["e_gate"], [4, 8, 128])
                pv, pg = S.ps(), S.ps()
                self.proj_fm(wts["v"][0], wts["v"][1], j % 4, 8, hrhs, pv)
                self.proj_fm(wts["g"][0], wts["g"][1], j % 4, 8, hrhs, pg)
                if not state_only:
                    dg = self.DG[j % 2]
                    cw = self.o_convw + j * 31
                    self.tt(dg.ap, dg.a(), self.identB.ap.unsqueeze(1).to_broadcast([128, 31, 128]), self.identB.a(),
                            p0[:, cw:cw + 31].unsqueeze(2).to_broadcast([128, 31, 128]), p0.a(), ALU.mult)
                return pv, pg

            def conv_mid(j, pv, pg):
                U = UB[j % 2]
                sg = SG[j % 2]
                self.act(sg.ap, sg.a(), pg.t[:, :], pg.a(), AF.Sigmoid)
                self.cp(U[:, 0:30], U.a(), UH[:, j, :], UH.a(), eng="dve")
                self.tt(U[:, 30:542], U.a(), pv.t[:, :], pv.a(), sg.ap, sg.a(), ALU.mult)
                self.cp(UH[:, j, :], UH.a(), U[:, 512:542], U.a(), eng="dve")

            def conv_back(j):
                U = UB[j % 2]
                dg = self.DG[j % 2]
                pc = S.ps()
                for k in range(31):
                    self.mm(pc.t[:, :], pc.a(), dg[:, k, :], dg.a(), U[:, k:k + 512], U.a(), k == 0, k == 30)
                self.act(YC[:, j, :], YC.a(j), pc.t[:, :], pc.a(), AF.Identity,
                         bias=p0[:, self.o_convb + j:self.o_convb + j + 1], extra_reads=[p0.a()])

            pend = conv_front(0)
            for j in range(8):
                conv_mid(j, *pend)
                if j + 1 < 8:
                    pend = conv_front(j + 1)
                if not state_only:
                    conv_back(j)
            if not state_only:
                self.ln_stats(lambda j: (YC[:, j, :], YC.a(j)), 8)
                for j in range(8):
                    yc = YC[:, j, :]
                    self.tt(yc, YC.a(j), yc, YC.a(j), self.RSTD.ap, self.RSTD.a(), ALU.mult)
                    self.tt(yc, YC.a(j), yc, YC.a(j), self.NMR.ap, self.NMR.a(), ALU.add)
                    self.act(UF[:, j, :], UF.a(j), yc, YC.a(j), AF.Silu,
                             bias=p0[:, self.o_clnb + j:self.o_clnb + j + 1], scale=p0[:, self.o_clng + j:self.o_clng + j + 1],
                             extra_reads=[p0.a()])

        wxs = {}

        def xbc_front(j):
            if j % 4 == 0:
                wxs["w"] = self.wload(self.W["e_xbc"].t[j:j + 4].rearrange("g p f -> p g f"), self.W["e_xbc"], [4, 8, 128])
            XR = XRB[j % 2]
            px = S.ps()
            self.proj_fm(wxs["w"][0], wxs["w"][1], j % 4, 8, hrhs, px)
            self.cp(XR[:, 0:3], XR.a(), XRH[:, j, :], XRH.a(), eng="dve")
            self.cp(XR[:, 3:515], XR.a(), px.t[:, :], px.a())
            self.cp(XRH[:, j, :], XRH.a(), XR[:, 512:515], XR.a(), eng="dve")
            dgx = self.DGX[j % 2]
            sw = self.o_scw + j * 4
            self.tt(dgx.ap, dgx.a(), self.identB.ap.unsqueeze(1).to_broadcast([128, 4, 128]), self.identB.a(),
                    p0[:, sw:sw + 4].unsqueeze(2).to_broadcast([128, 4, 128]), p0.a(), ALU.mult)

        def xbc_mid(j):
            XR = XRB[j % 2]
            dgx = self.DGX[j % 2]
            pcx = S.ps()
            for k in range(4):
                self.mm(pcx.t[:, :], pcx.a(), dgx[:, k, :], dgx.a(), XR[:, k:k + 512], XR.a(), k == 0, k == 3)
            scb = p0[:, self.o_scb + j:self.o_scb + j + 1]
            if j < 8:
                X = XST[j % 2]
                self.act(X.ap, X.a(), pcx.t[:, :], pcx.a(), AF.Silu, bias=scb, extra_reads=[p0.a()])
            elif j < 10:
                g = j - 8
                self.act(BT[:, g, :], BT.a(g), pcx.t[:, :], pcx.a(), AF.Silu, bias=scb, extra_reads=[p0.a()])
            else:
                g = j - 10
                self.act(CT[:, g, :], CT.a(g), pcx.t[:, :], pcx.a(), AF.Silu, bias=scb, extra_reads=[p0.a()])

        def xbc_back(j):
            if j < 8:
                X = XST[j % 2]
                pt = S.ps()
                for c in range(4):
                    self.tr(pt.t[:, c * 128:(c + 1) * 128], pt.a(), X[:, c * 128:(c + 1) * 128], X.a(),
                            (self.identF.ap, self.identF.a()))
                self.cp(XS[:, :, j * 128:(j + 1) * 128], XS.a(), pt.t[:, :].rearrange("p (c n) -> p c n", c=4), pt.a())
            elif j < 10:
                g = j - 8
                pt = S.ps()
                ptb = pt.t[:, :].bitcast(BF16)
                for c in range(4):
                    self.tr(ptb[:, c * 128:(c + 1) * 128], pt.a(), BT[:, g, c * 128:(c + 1) * 128], BT.a(g),
                            (self.identB.ap, self.identB.a()))
                self.cp(BTOK[:, :, g, :], BTOK.a(), ptb[:, 0:512].rearrange("p (c n) -> p c n", c=4), pt.a())

        nx = 10 if (state_only and not halo) else 12
        for step in range(nx + 2):
            if step < nx:
                xbc_front(step)
            if 0 <= step - 1 < nx:
                xbc_mid(step - 1)
            if 0 <= step - 2 < nx:
                xbc_back(step - 2)

        wzs = []
        if not state_only:
            wzs = [self.wload(self.W["e_z"].t[half], self.W["e_z"], [8, 512]) for half in range(2)]

        def z_proj(c):
            for half in range(2):
                wz, wza = wzs[half]
                pz = S.ps()
                for kc in range(8):
                    self.mm(pz.t[:, :], pz.a(), HT[:, kc, c * 128:(c + 1) * 128], HT.a(kc), wz[:, kc, :], wza, kc == 0, kc == 7)
                self.act(SZ[:, c, half * 512:(half + 1) * 512], SZ.a(c), pz.t[:, :], pz.a(), AF.Silu)
        DTV, ADT, SM = self.DTV, self.ADT, self.SM
        pdt = S.ps()
        for c in range(4):
            for kc in range(8):
                self.mm(pdt.t[:, c * 16:(c + 1) * 16], pdt.a(), HT[:, kc, c * 128:(c + 1) * 128], HT.a(kc),
                        self.wdt[:, kc, :], self.wdt.a(), kc == 0, kc == 7)
        dtb = p0[:, self.o_dtb:self.o_dtb + 16].unsqueeze(1).to_broadcast([128, 4, 16])
        self.tt(DTV.ap, DTV.a(), pdt.t[:, 0:64].rearrange("p (c h) -> p c h", c=4), pdt.a(), dtb, p0.a(), ALU.add)
        self.act(DTV.ap, DTV.a(), DTV.ap, DTV.a(), AF.Exp)
        self.ts(DTV.ap, DTV.a(), DTV.ap, DTV.a(), 1.0, None, ALU.add)
        self.act(DTV.ap, DTV.a(), DTV.ap, DTV.a(), AF.Ln)
        self.tt(ADT.ap, ADT.a(), DTV.ap, DTV.a(), self.AROW.ap.unsqueeze(1).to_broadcast([128, 4, 16]), self.AROW.a(), ALU.mult)

        HS, HBF = self.HS32, self.HBF

        def ssd_pre(c):
            SMc = self.SM[c % 2]
            ACS, NACS, EACS, TOT, DTE, CD, W2 = (SMc[:, i, :] for i in range(7))
            sma = SMc.a()
            pa = S.ps()
            self.mm(pa.t[:, 0:16], pa.a(), self.tri.ap, self.tri.a(), ADT[:, c, :], ADT.a())
            self.mm(pa.t[:, 16:32], pa.a(), self.onesF.ap, self.onesF.a(), ADT[:, c, :], ADT.a())
            self.cp(ACS, sma, pa.t[:, 0:16], pa.a(), eng="dve")
            self.cp(TOT, sma, pa.t[:, 16:32], pa.a(), eng="dve")
            self.ts(NACS, sma, ACS, sma, -1.0, None, ALU.mult)
            self.act(EACS, sma, ACS, sma, AF.Exp)
            self.tt(DTE, sma, TOT, sma, ACS, sma, ALU.subtract)
            self.act(DTE, sma, DTE, sma, AF.Exp)
            self.act(CD, sma, TOT, sma, AF.Exp)
            self.tt(W2, sma, DTV[:, c, :], DTV.a(), DTE, sma, ALU.mult)
            xs3 = XS[:, c, :].rearrange("p (h d) -> p h d", h=16)
            XWc = XWs[c % 2]
            self.tt(XWc.ap, XWc.a(), xs3, XS.a(c), W2.unsqueeze(2).to_broadcast([128, 16, 64]), sma, ALU.mult)

        ssd_pre(0)
        for c in range(4):
            SMc = self.SM[c % 2]
            ACS, NACS, EACS, TOT, DTE, CD, W2 = (SMc[:, i, :] for i in range(7))
            sma = SMc.a()
            XW = XWs[c % 2]
            xs3 = XS[:, c, :].rearrange("p (h d) -> p h d", h=16)
            if state_only and c + 1 < 4:
                ssd_pre(c + 1)
            if not state_only:
                z_proj(c)
                self.tt(XDT.ap, XDT.a(), xs3, XS.a(c), DTV[:, c, :].unsqueeze(2).to_broadcast([128, 16, 64]), DTV.a(), ALU.mult)
                for g in range(2):
                    pcb = S.ps()
                    self.mm(pcb.t[:, 0:128], pcb.a(), BT[:, g, c * 128:(c + 1) * 128], BT.a(g),
                            CT[:, g, c * 128:(c + 1) * 128], CT.a(g))
                    self.tt(DIAG.ap, DIAG.a(), self.identF.ap.unsqueeze(1).to_broadcast([128, 8, 128]), self.identF.a(),
                            ACS[:, g * 8:(g + 1) * 8].unsqueeze(2).to_broadcast([128, 8, 128]), sma, ALU.mult)
                    for half in range(2):
                        pd = S.ps()
                        self.mm(pd.t[:, :], pd.a(), self.onesF.ap, self.onesF.a(),
                                DIAG[:, half * 4:(half + 1) * 4, :].rearrange("p a b -> p (a b)"), DIAG.a(), True, False)
                        self.mm(pd.t[:, :], pd.a(), self.identB.ap, self.identB.a(),
                                self.negmask[:, half * 4:(half + 1) * 4, :].rearrange("p a b -> p (a b)"), self.negmask.a(), False, True)
                        for q in range(4):
                            hh = half * 4 + q
                            h = g * 8 + hh
                            self.act(LEXP[:, hh, :], LEXP.a(hh), pd.t[:, q * 128:(q + 1) * 128], pd.a(), AF.Exp,
                                     bias=NACS[:, h:h + 1], extra_reads=[sma])
                    self.tt(MT.ap, MT.a(), LEXP.ap, LEXP.a(),
                            pcb.t[:, 0:128].unsqueeze(1).to_broadcast([128, 8, 128]), pcb.a(), ALU.mult)
                    pyg = S.ps()
                    for hh in range(8):
                        h = g * 8 + hh
                        self.mm(pyg.t[:, hh * 64:(hh + 1) * 64], pyg.a(), MT[:, hh, :], MT.a(), XDT[:, h, :], XDT.a())
                    pog = S.ps()
                    self.mm(pog.t[:, :], pog.a(), CT[:, g, c * 128:(c + 1) * 128], CT.a(g), HBF[:, g * 512:(g + 1) * 512], HBF.a())
                    ysl = Y32[:, g * 512:(g + 1) * 512]
                    self.tt(ysl.rearrange("p (h d) -> p h d", h=8), Y32.a(), pog.t[:, :].rearrange("p (h d) -> p h d", h=8), pog.a(),
                            EACS[:, g * 8:(g + 1) * 8].unsqueeze(2).to_broadcast([128, 8, 64]), sma, ALU.mult)
                    self.tt(ysl, Y32.a(), ysl, Y32.a(), pyg.t[:, :], pyg.a(), ALU.add)
                if c + 1 < 4:
                    ssd_pre(c + 1)
                dsk = p0[:, self.o_dsk:self.o_dsk + 16].unsqueeze(2).to_broadcast([128, 16, 64])
                self.tt(TMP.ap.rearrange("p (h d) -> p h d", h=16), TMP.a(), xs3, XS.a(c), dsk, p0.a(), ALU.mult)
                self.tt(Y32.ap, Y32.a(), Y32.ap, Y32.a(), TMP.ap, TMP.a(), ALU.add)
            if not state_only and c == 0:
                self.dbg("xs", XS[:, 0, :], XS.a(0), [1024])
                self.dbg("dtv", DTV[:, 0, :], DTV.a(), [16])
                self.dbg("acs", ACS, sma, [16])
                self.dbg("y32", Y32.ap, Y32.a(), [1024])
                self.dbg("sz", SZ[:, 0, :], SZ.a(0), [1024])
                self.dbg("mt", MT.ap, MT.a(), [8, 128])
                self.dbg("lexp", LEXP.ap, LEXP.a(), [8, 128])
            def state_update():
                pst = [S.ps(), S.ps()]
                for g in range(2):
                    self.mm(pst[g].t[:, :], pst[g].a(), BTOK[:, c, g, :], BTOK.a(),
                            XW[:, g * 8:(g + 1) * 8, :].rearrange("p a b -> p (a b)"), XW.a())
                for g in range(2):
                    hs = HS[:, g * 512:(g + 1) * 512]
                    self.tt(hs.rearrange("p (h d) -> p h d", h=8), HS.a(), hs.rearrange("p (h d) -> p h d", h=8), HS.a(),
                            CD[:, g * 8:(g + 1) * 8].unsqueeze(2).to_broadcast([128, 8, 64]), sma, ALU.mult)
                    self.tt(hs, HS.a(), hs, HS.a(), pst[g].t[:, :], pst[g].a(), ALU.add)
                self.cp(HBF.ap, HBF.a(), HS.ap, HS.a())

            if state_only:
                state_update()
                continue
            self.tt(Y32.ap, Y32.a(), Y32.ap, Y32.a(), SZ[:, c, :], SZ.a(c), ALU.mult)
            SSv = self.SS
            self.tt(TMP.ap, TMP.a(), Y32.ap, Y32.a(), Y32.ap, Y32.a(), ALU.mult)
            S.op("dve", lambda e: e.tensor_reduce(out=SSv[:, 0:2], in_=TMP.ap.rearrange("p (g n) -> p g n", g=2),
                                                  axis=mybir.AxisListType.X, op=ALU.add), reads=[TMP.a()], writes=[SSv.a()])
            self.ts(SSv[:, 0:2], SSv.a(), SSv[:, 0:2], SSv.a(), 1.0 / 512, EPS, ALU.mult, ALU.add)
            self.act(SSv[:, 0:2], SSv.a(), SSv[:, 0:2], SSv.a(), AF.Sqrt)
            S.op("dve", lambda e: e.reciprocal(out=SSv[:, 2:4], in_=SSv[:, 0:2]), reads=[SSv.a()], writes=[SSv.a()])
            for g in range(2):
                ysl = Y32[:, g * 512:(g + 1) * 512]
                self.stt(YN[:, g * 512:(g + 1) * 512], YN.a(), ysl, Y32.a(), SSv[:, 2 + g:3 + g],
                         p0[:, self.o_normg + g * 512:self.o_normg + (g + 1) * 512], p0.a(), ALU.mult, ALU.mult,
                         extra_reads=[SSv.a()])
            pt = S.ps()
            ptb = pt.t[:, :].bitcast(BF16)
            for j in range(8):
                self.tr(ptb[:, j * 128:(j + 1) * 128], pt.a(), YN[:, j * 128:(j + 1) * 128], YN.a(),
                        (self.identB.ap, self.identB.a()))
            self.cp(YT[:, :, c * 128:(c + 1) * 128], YT.a(), ptb.rearrange("p (j n) -> p j n", j=8), pt.a())
            state_update()
        if state_only:
            return
        mix = lambda kc: (UF[:, kc, :], UF.a(kc)) if kc < 8 else (YT[:, kc - 8, :], YT.a(kc - 8))
        self.out_proj_resid("e_out", 16, mix, 2)
        self.ln_resid(0, 0)

    def xattn(self, L):
        S, Tm, HT = self.S, self.Tm, self.HT
        Tm.top = 0
        QT = Tm.alloc([8, 512], BF16)
        OT = Tm.alloc([8, 512], BF16)
        PT = [Tm.alloc([512], BF16) for _ in range(4)]
        RDEN = Tm.alloc([512], F32)
        KT, VV = self.KT[L], self.VV[L]
        hrhs = lambda kc: (HT[:, kc, :], HT.a(kc))
        wq = self.W[f"xq{L}"]
        for j in range(8):
            if j % 4 == 0:
                w, wa = self.wload(wq.t[j:j + 4].rearrange("g p f -> p g f"), wq, [4, 8, 128])
            pq = S.ps()
            self.proj_fm(w, wa, j % 4, 8, hrhs, pq)
            S.op("act", lambda e, j=j, pq=pq: e.mul(QT[:, j, :], pq.t[:, :], 1.0 / 16.0), reads=[pq.a()], writes=[QT.a(j)])
        def scores(hd):
            pts = [PT[(hd % 2) * 2], PT[(hd % 2) * 2 + 1]]
            for mc in range(2):
                psc = S.ps()
                for dc in range(2):
                    j = 2 * hd + dc
                    self.mm(psc.t[:, :], psc.a(), KT[:, j, mc * 128:(mc + 1) * 128], KT.a(j), QT[:, j, :], QT.a(j), dc == 0, dc == 1)
                self.act(pts[mc].ap, pts[mc].a(), psc.t[:, :], psc.a(), AF.Exp)

        def den_pv(hd):
            pts = [PT[(hd % 2) * 2], PT[(hd % 2) * 2 + 1]]
            pden = S.ps()
            for mc in range(2):
                self.mm(pden.t[:, :], pden.a(), self.onesB.ap, self.onesB.a(), pts[mc].ap, pts[mc].a(), mc == 0, mc == 1)
            self.act(RDEN.ap, RDEN.a(), pden.t[:, :], pden.a(), AF.Ln)
            self.act(RDEN.ap, RDEN.a(), RDEN.ap, RDEN.a(), AF.Exp, scale=-1.0)
            for dc in range(2):
                j = 2 * hd + dc
                pov = S.ps()
                for mc in range(2):
                    self.mm(pov.t[:, :], pov.a(), VV[:, mc, j * 128:(j + 1) * 128], VV.a(mc), pts[mc].ap, pts[mc].a(), mc == 0, mc == 1)
                self.tt(OT[:, j, :], OT.a(j), pov.t[:, :], pov.a(), RDEN.ap, RDEN.a(), ALU.mult)

        scores(0)
        for hd in range(4):
            if hd + 1 < 4:
                scores(hd + 1)
            den_pv(hd)
        self.out_proj_resid(f"xo{L}", 8, lambda kc: (OT[:, kc, :], OT.a(kc)), 4)
        self.ln_resid(L, 1)

    def mlp(self, L):
        S, Tm, HT = self.S, self.Tm, self.HT
        Tm.top = 0
        H1 = Tm.alloc([32, 512], BF16)
        RL = [Tm.alloc([512], F32) for _ in range(4)]
        hrhs = lambda kc: (HT[:, kc, :], HT.a(kc))
        w1 = self.W[f"w1{L}"]
        for jf in range(32):
            if jf % 4 == 0:
                w, wa = self.wload(w1.t[jf:jf + 4].rearrange("g p f -> p g f"), w1, [4, 8, 128])
            pf = S.ps()
            self.proj_fm(w, wa, jf % 4, 8, hrhs, pf)
            r = RL[jf % 4]
            self.act(r.ap, r.a(), pf.t[:, :], pf.a(), AF.Relu)
            self.act(H1[:, jf, :], H1.a(jf), r.ap, r.a(), AF.Square)
        self.out_proj_resid(f"w2{L}", 32, lambda kc: (H1[:, kc, :], H1.a(kc)), 1)
        self.ln_resid(L, 2)

    def mixer1(self, state_only=False):
        S, Tm, HT = self.S, self.Tm, self.HT
        p1 = self.prm1
        Tm.top = 0
        QKT = Tm.alloc([8, 512], BF16)
        QTT = Tm.alloc([4, 512], BF16)
        KTT = Tm.alloc([4, 512], BF16)
        KEN = Tm.alloc([4, 512], BF16)
        KETOK = Tm.alloc([4, 4, 128], BF16)
        VTOK = Tm.alloc([4, 1024], BF16)
        SGO = Tm.alloc([4, 1024], BF16)
        EBL = Tm.alloc([4, 4], F32)
        GLT = Tm.alloc([512], BF16)
        BCS = Tm.alloc([512], F32)
        E1 = Tm.alloc([512], F32)
        OTF = Tm.alloc([8, 512], BF16)
        ATT = Tm.alloc([128], BF16)
        O32 = Tm.alloc([1024], F32)
        TMP = Tm.alloc([1024], F32)
        ON = Tm.alloc([1024], BF16)
        LGS = Tm.alloc([4, 512], F32, at=O32.off)
        hrhs = lambda kc: (HT[:, kc, :], HT.a(kc))
        GS, GSB = self.GS32, self.GSBF
        wqk = self.W["o_qk"]
        for j in range(8):
            if j % 4 == 0:
                w, wa = self.wload(wqk.t[j:j + 4].rearrange("g p f -> p g f"), wqk, [4, 8, 128])
            if state_only and j < 4:
                continue
            pq = S.ps()
            self.proj_fm(w, wa, j % 4, 8, hrhs, pq)
            self.cp(QKT[:, j, :], QKT.a(j), pq.t[:, :], pq.a())
        pgl = S.ps()
        for kc in range(8):
            self.mm(pgl.t[0:16, :], pgl.a(), self.wgl[:, kc, :], self.wgl.a(), HT[:, kc, :], HT.a(kc), kc == 0, kc == 7)
        self.cp(GLT[0:16, :], GLT.a(), pgl.t[0:16, :], pgl.a())
        wvg = self.W["o_vg"]

        def vg_proj(half):
            wz, wza = self.wload(wvg.t[half], wvg, [8, 512])
            for c in range(4):
                pz = S.ps()
                for kc in range(8):
                    self.mm(pz.t[:, :], pz.a(), HT[:, kc, c * 128:(c + 1) * 128], HT.a(kc), wz[:, kc, :], wza, kc == 0, kc == 7)
                if half < 2:
                    self.cp(VTOK[:, c, half * 512:(half + 1) * 512], VTOK.a(c), pz.t[:, :], pz.a())
                else:
                    self.act(SGO[:, c, (half - 2) * 512:(half - 1) * 512], SGO.a(c), pz.t[:, :], pz.a(), AF.Silu)

        for h in range(4):
            pg = S.ps()
            self.mm(pg.t[:, :], pg.a(), self.wg2[0:16, h * 128:(h + 1) * 128], self.wg2.a(), GLT[0:16, :], GLT.a())
            self.ts(LGS[:, h, :], LGS.a(h), pg.t[:, :], pg.a(), p1[:, self.o_bg + h:self.o_bg + h + 1], -1.0, ALU.add, ALU.mult,
                    extra_reads=[p1.a()])
        for h in range(4):
            LGh = LGS[:, h, :]
            la = LGS.a(h)
            self.act(LGh, la, LGh, la, AF.Exp)
            self.ts(LGh, la, LGh, la, 1.0, None, ALU.add)
            self.act(LGh, la, LGh, la, AF.Ln)
            self.ts(LGh, la, LGh, la, -1.0 / 16.0, None, ALU.mult)
            S.op("dve", lambda e, LGh=LGh: e.tensor_tensor_scan(out=BCS.ap, data0=self.rmask.ap, data1=LGh, initial=0.0,
                                                                op0=ALU.mult, op1=ALU.add),
                 reads=[self.rmask.a(), la], writes=[BCS.a()])
            if not state_only:
                self.act(E1.ap, E1.a(), BCS.ap, BCS.a(), AF.Exp)
                self.stt(QTT[:, h, :], QTT.a(h), QKT[:, h, :], QKT.a(h), float(128 ** -0.5), E1.ap, E1.a(), ALU.mult, ALU.mult)
                self.act(E1.ap, E1.a(), BCS.ap, BCS.a(), AF.Exp, scale=-1.0)
                self.tt(KTT[:, h, :], KTT.a(h), QKT[:, 4 + h, :], QKT.a(4 + h), E1.ap, E1.a(), ALU.mult)
            bl = BCS[:, 127:512:128]
            self.act(EBL[:, h, :], EBL.a(), bl, BCS.a(), AF.Exp)
            self.tt(E1.ap.rearrange("p (c n) -> p c n", c=4), E1.a(), bl.unsqueeze(2).to_broadcast([128, 4, 128]), BCS.a(),
                    BCS.ap.rearrange("p (c n) -> p c n", c=4), BCS.a(), ALU.subtract)
            self.act(E1.ap, E1.a(), E1.ap, E1.a(), AF.Exp)
            self.tt(KEN[:, h, :], KEN.a(h), QKT[:, 4 + h, :], QKT.a(4 + h), E1.ap, E1.a(), ALU.mult)
            if h < (2 if state_only else 4):
                vg_proj(h)
            pt = S.ps()
            ptb = pt.t[:, :].bitcast(BF16)
            for c in range(4):
                self.tr(ptb[:, c * 128:(c + 1) * 128], pt.a(), KEN[:, h, c * 128:(c + 1) * 128], KEN.a(h),
                        (self.identB.ap, self.identB.a()))
            self.cp(KETOK[:, :, h, :], KETOK.a(), ptb[:, 0:512].rearrange("p (c n) -> p c n", c=4), pt.a())
        for c in range(4):
            if not state_only:
                po = [S.ps(), S.ps()]
                for h in range(4):
                    pat = S.ps()
                    self.mm(pat.t[:, 0:128], pat.a(), KTT[:, h, c * 128:(c + 1) * 128], KTT.a(h),
                            QTT[:, h, c * 128:(c + 1) * 128], QTT.a(h))
                    self.tt(ATT.ap, ATT.a(), pat.t[:, 0:128], pat.a(), self.tri.ap, self.tri.a(), ALU.mult)
                    osl = po[h // 2].t[:, (h % 2) * 256:(h % 2 + 1) * 256]
                    self.mm(osl, po[h // 2].a(), ATT.ap, ATT.a(), VTOK[:, c, h * 256:(h + 1) * 256], VTOK.a(c), True, False)
                    self.mm(osl, po[h // 2].a(), QTT[:, h, c * 128:(c + 1) * 128], QTT.a(h), GSB[:, h * 256:(h + 1) * 256], GSB.a(),
                            False, True)
            if not state_only:
                for g in range(2):
                    self.cp(O32[:, g * 512:(g + 1) * 512], O32.a(), po[g].t[:, :], po[g].a())
            pst = [S.ps(), S.ps()]
            for h in range(4):
                self.mm(pst[h // 2].t[:, (h % 2) * 256:(h % 2 + 1) * 256], pst[h // 2].a(), KETOK[:, c, h, :], KETOK.a(),
                        VTOK[:, c, h * 256:(h + 1) * 256], VTOK.a(c))
            for h in range(4):
                gs = GS[:, h * 256:(h + 1) * 256]
                self.stt(gs, GS.a(), gs, GS.a(), EBL[:, h, c:c + 1], pst[h // 2].t[:, (h % 2) * 256:(h % 2 + 1) * 256], pst[h // 2].a(),
                         ALU.mult, ALU.add, extra_reads=[EBL.a()])
            self.cp(GSB.ap, GSB.a(), GS.ap, GS.a())
            if state_only:
                continue
            SSv = self.SS1
            self.tt(TMP.ap, TMP.a(), O32.ap, O32.a(), O32.ap, O32.a(), ALU.mult)
            S.op("dve", lambda e: e.tensor_reduce(out=SSv[:, 0:4], in_=TMP.ap.rearrange("p (g n) -> p g n", g=4),
                                                  axis=mybir.AxisListType.X, op=ALU.add), reads=[TMP.a()], writes=[SSv.a()])
            self.ts(SSv[:, 0:4], SSv.a(), SSv[:, 0:4], SSv.a(), 1.0 / 256, EPS, ALU.mult, ALU.add)
            self.act(SSv[:, 0:4], SSv.a(), SSv[:, 0:4], SSv.a(), AF.Sqrt)
            S.op("dve", lambda e: e.reciprocal(out=SSv[:, 4:8], in_=SSv[:, 0:4]), reads=[SSv.a()], writes=[SSv.a()])
            for h in range(4):
                osl = O32[:, h * 256:(h + 1) * 256]
                self.stt(osl, O32.a(), osl, O32.a(), SSv[:, 4 + h:5 + h], p1[:, self.o_hng:self.o_hng + 256], p1.a(),
                         ALU.mult, ALU.mult, extra_reads=[SSv.a()])
            self.tt(ON.ap, ON.a(), O32.ap, O32.a(), SGO[:, c, :], SGO.a(c), ALU.mult)
            pt = S.ps()
            ptb = pt.t[:, :].bitcast(BF16)
            for j in range(8):
                self.tr(ptb[:, j * 128:(j + 1) * 128], pt.a(), ON[:, j * 128:(j + 1) * 128], ON.a(),
                        (self.identB.ap, self.identB.a()))
            self.cp(OTF[:, :, c * 128:(c + 1) * 128], OTF.a(), ptb.rearrange("p (j n) -> p j n", j=8), pt.a())
        if state_only:
            return
        self.out_proj_resid("o_out", 8, lambda kc: (OTF[:, kc, :], OTF.a(kc)), 4)
        self.ln_resid(1, 0)


def prep_weights(inp, layers):
    f = lambda a: np.ascontiguousarray(a, dtype=np.float32)
    W = {}
    if 0 in layers:
        w_in = inp["even_w_in"][0]
        W["e_val"] = tile_lhsT(w_in[:, 0:1024])
        W["e_gate"] = tile_lhsT(w_in[:, 1024:2048])
        W["e_z"] = tile_rhs(w_in[:, 2048:3072])
        W["e_xbc"] = tile_lhsT(w_in[:, 3072:4608])
        W["e_dt"] = f(w_in[:, 4608:4624].reshape(8, 128, 16).transpose(1, 0, 2).reshape(128, 128))
        W["e_out"] = tile_lhsT(inp["even_w_out"][0])
        prm = np.zeros((128, 1460), np.float32)
        prm[:, 0:248] = inp["even_conv_w"][0].T.reshape(8, 128, 31).transpose(1, 0, 2).reshape(128, 248)
        prm[:, 248:256] = chan(inp["even_conv_b"][0])
        prm[:, 256:264] = chan(inp["even_conv_ln_g"][0])
        prm[:, 264:272] = chan(inp["even_conv_ln_b"][0])
        prm[:, 272:320] = inp["even_ssm_conv_w"][0].T.reshape(12, 128, 4).transpose(1, 0, 2).reshape(128, 48)
        prm[:, 320:332] = chan(inp["even_ssm_conv_b"][0])
        prm[:, 332:348] = rows(inp["even_dt_bias"][0])
        prm[:, 348:364] = rows(inp["even_a_log"][0])
        prm[:, 364:380] = rows(inp["even_d_skip"][0])
        prm[:, 380:1404] = rows(inp["even_ssm_norm_g"][0])
        for i in range(3):
            prm[:, 1404 + i * 8:1412 + i * 8] = chan(inp["ln_g"][0, i])
            prm[:, 1428 + i * 8:1436 + i * 8] = chan(inp["ln_b"][0, i])
        W["prm0"] = prm
    if 1 in layers:
        w_in = inp["odd_w_in"][0]
        W["o_qk"] = tile_lhsT(w_in[:, 0:1024])
        W["o_vg"] = tile_rhs(w_in[:, 1024:3072])
        W["o_gl"] = f(w_in[:, 3072:3088].reshape(8, 128, 16).transpose(1, 0, 2).reshape(128, 128))
        W["o_g2"] = f(inp["odd_w_gate2"][0])
        W["o_out"] = tile_lhsT(inp["odd_w_out"][0])
        prm = np.zeros((128, 320), np.float32)
        prm[:, 0:4] = chan(inp["odd_b_gate"][0])
        prm[:, 4:260] = rows(inp["odd_head_norm_g"][0])
        for i in range(3):
            prm[:, 260 + i * 8:268 + i * 8] = chan(inp["ln_g"][1, i])
            prm[:, 284 + i * 8:292 + i * 8] = chan(inp["ln_b"][1, i])
        W["prm1"] = prm
    for L in layers:
        W[f"xq{L}"] = tile_lhsT(inp["xa_w_q"][L])
        W[f"xk{L}"] = tile_lhsT(inp["xa_w_k"][L])
        W[f"xv{L}"] = tile_rhs(inp["xa_w_v"][L])
        W[f"xo{L}"] = tile_lhsT(inp["xa_w_o"][L])
        W[f"w1{L}"] = tile_lhsT(inp["mlp_w1"][L])
        W[f"w2{L}"] = tile_lhsT(inp["mlp_w2"][L])
    return W


_PROGS = {}


STAGES = ("mix", "xa", "mlp")
DEBUG = False
SAME_ENGINE_SYNC = True


def get_prog(layers, NT, NP):
    key = (tuple(layers), NT, NP, STAGES)
    if key not in _PROGS:
        _PROGS[key] = Prog(tuple(layers), NT, NP, same_engine_sync=SAME_ENGINE_SYNC, stages=STAGES)
    return _PROGS[key]


def run_layers(layers, h, mem, inp, W=None):
    B, L, _ = h.shape
    half = L // 2
    NT = half // T
    prog = get_prog(layers, NT, NT)
    if W is None:
        W = prep_weights(inp, layers)
    in_maps = []
    for core in range(2 * B):
        b, s = core // 2, core % 2
        m = dict(W)
        m["xT"] = np.ascontiguousarray(h[b, s * half:(s + 1) * half].T)
        if s == 0:
            m["xpT"] = np.zeros((D, half), np.float32)
        else:
            m["xpT"] = np.ascontiguousarray(h[b, 0:half].T)
        m["memT"] = np.ascontiguousarray(mem[b].T)
        m["flag"] = np.full((128, 1), float(s), np.float32)
        in_maps.append(m)
    res = run_bass_kernel_spmd(prog.nc, in_maps, core_ids=list(range(2 * B)))
    out = np.empty((B, L, D), np.float32)
    for core in range(2 * B):
        b, s = core // 2, core % 2
        out[b, s * half:(s + 1) * half] = res.results[core]["outT"].T
    return out


def kernel(**inputs):
    inp = {k: np.asarray(v) for k, v in inputs.items()}
    h = np.ascontiguousarray(inp["x"], dtype=np.float32)
    mem = np.ascontiguousarray(inp["mem"], dtype=np.float32)
    return run_layers((0, 1), h, mem, inp)
```

```python
import math
import numpy as np
import concourse.bass as bass
import concourse.mybir as mybir
from concourse.bass_utils import run_bass_kernel_spmd

F32 = mybir.dt.float32
BF16 = mybir.dt.bfloat16
AF = mybir.ActivationFunctionType
ALU = mybir.AluOpType

D = 1024
T = 512
ALPHA = float((2 * 2) ** 0.25)
EPS = 1e-5
NCORES = 8
ENGINES = ("pe", "act", "dve", "pool", "sp")
GR = 128


class Buf:
    def __init__(self, name, t, nparts=1):
        self.name = name
        self.t = t
        self.nparts = nparts
        self.last_w = [None] * nparts
        self.readers = [[] for _ in range(nparts)]
        self.dma_sem = None
        self.ndma = 0
        self.sem_inc = 16

    def a(self, p=None):
        if p is None:
            return (self, range(self.nparts))
        return (self, (p,))


class Op:
    __slots__ = ("eng", "fn", "deps", "is_dma", "sig_sem", "sig_val", "needs_sig", "idx")


class View:
    def __init__(self, buf, off, words, ap, shape, bpe):
        self.buf, self.off, self.words, self.ap, self.shape, self.bpe = buf, off, words, ap, shape, bpe
        self.strides = []
        s = 1
        for d in reversed(shape):
            self.strides.insert(0, s)
            s *= d

    def __getitem__(self, idx):
        return self.ap[idx]

    def a(self, *idx):
        e0 = 0
        n = self.strides[0] * self.shape[0]
        for k, i in enumerate(idx):
            e0 += i * self.strides[k]
            n = self.strides[k]
        w0 = self.off + (e0 * self.bpe) // 4
        w1 = self.off + (((e0 + n) * self.bpe + 3) // 4)
        return (self.buf, range(w0 // GR, (w1 - 1) // GR + 1))


class Arena:
    def __init__(self, S, name, nwords):
        nwords = ((nwords + GR - 1) // GR) * GR
        self.buf = S.sbuf(name, [128, nwords], F32, nparts=nwords // GR)
        self.nwords = nwords
        self.top = 0

    def alloc(self, shape, dtype, at=None):
        n = 1
        for d in shape:
            n *= d
        bpe = 4 if dtype == F32 else 2
        words = (n * bpe + 3) // 4
        if at is None:
            off = self.top
            self.top = ((off + words + GR - 1) // GR) * GR
        else:
            off = at
        assert off + words <= self.nwords, (self.buf.name, off, words, self.nwords)
        ap = self.buf.t[:, off:off + words]
        if dtype != F32:
            ap = ap.bitcast(dtype)
        if len(shape) == 2:
            ap = ap.rearrange("p (a b) -> p a b", a=shape[0])
        elif len(shape) == 3:
            ap = ap.rearrange("p (a b c) -> p a b c", a=shape[0], b=shape[1])
        return View(self.buf, off, words, ap, list(shape), bpe)


class Sched:
    def __init__(self, nc, same_engine_sync=True):
        self.nc = nc
        self.ops = []
        self.same_engine_sync = same_engine_sync
        self.eng_count = {e: 0 for e in ENGINES}
        self.dma_bufs = []
        self.ctx = []
        self.banks = []
        self.bank_i = 0

    def sbuf(self, name, shape, dtype, nparts=1):
        cm = self.nc.sbuf_tensor(name, list(shape), dtype)
        t = cm.__enter__()
        self.ctx.append(cm)
        return Buf(name, t, nparts)

    def psum(self, name, shape, dtype, nparts=1):
        cm = self.nc.psum_tensor(name, list(shape), dtype)
        t = cm.__enter__()
        self.ctx.append(cm)
        return Buf(name, t, nparts)

    def dram(self, name, shape, dtype, kind="Internal"):
        t = self.nc.dram_tensor(name, list(shape), dtype, kind=kind)
        return Buf(name, t.ap(), 1)

    def ps(self):
        b = self.banks[self.bank_i % len(self.banks)]
        self.bank_i += 1
        return b

    def op(self, eng, fn, reads=(), writes=(), dma_buf=None):
        o = Op()
        o.eng, o.fn, o.idx = eng, fn, len(self.ops)
        o.is_dma = dma_buf is not None
        o.needs_sig = False
        deps = set()
        for b, parts in reads:
            for p in parts:
                if b.last_w[p] is not None:
                    deps.add(b.last_w[p])
        for b, parts in writes:
            for p in parts:
                if b.last_w[p] is not None:
                    deps.add(b.last_w[p])
                deps.update(b.readers[p])
        for b, parts in reads:
            for p in parts:
                b.readers[p].append(o.idx)
        for b, parts in writes:
            for p in parts:
                b.last_w[p] = o.idx
                b.readers[p] = []
        waits = {}
        for d in deps:
            do = self.ops[d]
            if do.is_dma:
                key = ("dma", do.sig_sem)
                val = do.sig_sem.ndma * do.sig_sem.sem_inc
            else:
                if do.eng == eng and not o.is_dma and (eng == "pe" or not self.same_engine_sync):
                    continue
                key = ("eng", do.eng)
                val = do.sig_val
            do.needs_sig = True
            if waits.get(key, 0) < val:
                waits[key] = val
        o.deps = waits
        if o.is_dma:
            if dma_buf.dma_sem is None:
                dma_buf.dma_sem = True
                self.dma_bufs.append(dma_buf)
            dma_buf.ndma += 1
            o.sig_sem = dma_buf
            o.sig_val = dma_buf.ndma * dma_buf.sem_inc
        else:
            self.eng_count[eng] += 1
            o.sig_sem = None
            o.sig_val = self.eng_count[eng]
        self.ops.append(o)
        return o

    def emit(self, final_wait_bufs=()):
        nc = self.nc
        remap = {e: {} for e in ENGINES}
        cnt = {e: 0 for e in ENGINES}
        for o in self.ops:
            if not o.is_dma and o.needs_sig:
                cnt[o.eng] += 1
                remap[o.eng][o.sig_val] = cnt[o.eng]
        sems = {}
        for e in ENGINES:
            cm = nc.semaphore("sem_" + e)
            sems[("eng", e)] = cm.__enter__()
            self.ctx.append(cm)
        for b in self.dma_bufs:
            cm = nc.semaphore("dsem_" + b.name)
            sems[("dma", b)] = cm.__enter__()
            self.ctx.append(cm)
        by_eng = {e: [o for o in self.ops if o.eng == e] for e in ENGINES}
        engobj = {"pe": "tensor", "act": "scalar", "dve": "vector", "pool": "gpsimd", "sp": "sync"}
        stats = {}
        with nc.Block() as block:
            for e in ENGINES:
                ops = by_eng[e]

                def body(eng, ops=ops, e=e):
                    waited = {}
                    nw = 0
                    for o in ops:
                        for key, val in o.deps.items():
                            if key[0] == "eng":
                                val = remap[key[1]][val]
                            if waited.get(key, 0) >= val:
                                continue
                            waited[key] = val
                            eng.wait_ge(sems[key], val)
                            nw += 1
                        ins = o.fn(eng)
                        if o.is_dma:
                            if o.sig_sem.sem_inc == 16:
                                ins.then_inc(sems[("dma", o.sig_sem)], 16)
                            else:
                                ins.then_inc(sems[("dma", o.sig_sem)])
                        elif o.needs_sig:
                            ins.then_inc(sems[("eng", e)], 1)
                    if e == "sp":
                        for b in final_wait_bufs:
                            eng.wait_ge(sems[("dma", b)], b.ndma * b.sem_inc)
                    stats[e] = (len(ops), nw)

                getattr(block, engobj[e])(body)
        self.stats = stats
        return stats


def tile_lhsT(W):
    K, N = W.shape
    kc, nj = K // 128, N // 128
    return np.ascontiguousarray(W.reshape(kc, 128, nj, 128).transpose(2, 1, 0, 3).reshape(nj, 128, kc * 128))


def tile_rhs(W, ncol=512):
    K, N = W.shape
    kc, ng = K // 128, N // ncol
    return np.ascontiguousarray(W.reshape(kc, 128, ng, ncol).transpose(2, 1, 0, 3).reshape(ng, 128, kc * ncol))


def chan(v):
    return np.ascontiguousarray(v.reshape(-1, 128).T)


def rows(v):
    return np.ascontiguousarray(np.broadcast_to(v.reshape(1, -1), (128, v.size)))


class Prog:
    def __init__(self, layers, NT, NP, same_engine_sync=True, stages=("mix", "xa", "mlp")):
        self.layers, self.NT, self.NP = layers, NT, NP
        self.stages = stages
        self.debug = DEBUG
        self.dbg_out = {}
        self.nc = bass.Bass("TRN2", target_bir_lowering=False)
        self.S = Sched(self.nc, same_engine_sync)
        self.inputs = {}
        self.build()

    def din(self, name, shape, dtype=F32):
        b = self.S.dram(name, shape, dtype, kind="ExternalInput")
        self.inputs[name] = b
        return b

    def mm(self, out_ap, out_acc, lhsT, lacc, rhs, racc, start=True, stop=True):
        self.S.op("pe", lambda e: e.matmul(out_ap, lhsT=lhsT, rhs=rhs, start=start, stop=stop),
                  reads=[lacc, racc], writes=[out_acc])

    def tr(self, out_ap, out_acc, in_ap, in_acc, ident):
        self.S.op("pe", lambda e: e.transpose(out_ap, in_ap, ident[0]), reads=[in_acc, ident[1]], writes=[out_acc])

    def act(self, out, oacc, in_, iacc, func, bias=None, scale=None, extra_reads=()):
        kw = {}
        if bias is not None:
            kw["bias"] = bias
        if scale is not None:
            kw["scale"] = scale
        self.S.op("act", lambda e: e.activation(out=out, in_=in_, func=func, **kw),
                  reads=[iacc] + list(extra_reads), writes=[oacc])

    def tt(self, out, oacc, in0, a0, in1, a1, op, eng="dve"):
        self.S.op(eng, lambda e: e.tensor_tensor(out=out, in0=in0, in1=in1, op=op), reads=[a0, a1], writes=[oacc])

    def ts(self, out, oacc, in0, a0, s1, s2, op0, op1=None, extra_reads=(), eng="dve"):
        if op1 is None:
            self.S.op(eng, lambda e: e.tensor_scalar(out=out, in0=in0, scalar1=s1, scalar2=None, op0=op0),
                      reads=[a0] + list(extra_reads), writes=[oacc])
        else:
            self.S.op(eng, lambda e: e.tensor_scalar(out=out, in0=in0, scalar1=s1, scalar2=s2, op0=op0, op1=op1),
                      reads=[a0] + list(extra_reads), writes=[oacc])

    def stt(self, out, oacc, in0, a0, scalar, in1, a1, op0, op1, extra_reads=()):
        self.S.op("dve", lambda e: e.scalar_tensor_tensor(out=out, in0=in0, scalar=scalar, in1=in1, op0=op0, op1=op1),
                  reads=[a0, a1] + list(extra_reads), writes=[oacc])

    def cp(self, out, oacc, in_, iacc, eng="act"):
        if eng == "act":
            self.S.op("act", lambda e: e.copy(out=out, in_=in_), reads=[iacc], writes=[oacc])
        else:
            self.S.op(eng, lambda e: e.tensor_copy(out=out, in_=in_), reads=[iacc], writes=[oacc])

    def dbg(self, name, ap, acc, shape):
        if not getattr(self, "debug", False) or name in self.dbg_out:
            return
        d = self.S.dram("dbg_" + name, [128] + list(shape), F32, kind="ExternalOutput")
        stg = self.S.sbuf("dbgs_" + name, [128] + list(shape), F32)
        self.S.op("dve", lambda e: e.tensor_copy(out=stg.t[:], in_=ap), reads=[acc], writes=[stg.a()])
        self.S.op("sp", lambda e: e.dma_start(out=d.t, in_=stg.t[:]), reads=[stg.a()], writes=[d.a()], dma_buf=stg)
        self.dbg_out[name] = (d, stg)

    def wload(self, src_ap, src_buf, shape):
        sl = self.slots[self.slot_i % len(self.slots)]
        self.slot_i += 1
        n = 1
        for d in shape:
            n *= d
        flat = sl.t[:, 0:n]
        if len(shape) == 3:
            dma_dst = flat.rearrange("p (a b) -> p a b", a=shape[0])
            view = flat.rearrange("p (a b c) -> p a b c", a=shape[0], b=shape[1])
        else:
            dma_dst = flat
            view = flat.rearrange("p (a b) -> p a b", a=shape[0])
        self.S.op("pool", lambda e: e.dma_start(out=dma_dst, in_=src_ap), reads=[src_buf.a()], writes=[sl.a()], dma_buf=sl)
        return view, sl.a()

    def build(self):
        S, nc = self.S, self.nc
        NT, NP = self.NT, self.NP
        NTOK, NPREV = NT * T, max(NP, 1) * T
        L0, L1 = 0 in self.layers, 1 in self.layers
        self.xT = self.din("xT", [D, NTOK])
        self.xpT = self.din("xpT", [D, NPREV])
        self.memT_d = self.din("memT", [D, 256])
        self.flag_d = self.din("flag", [128, 1])
        self.outT = S.dram("outT", [D, NTOK], F32, kind="ExternalOutput")
        NPRM0, NPRM1 = 1460, 320
        W = {}
        if L0:
            W["e_val"] = self.din("e_val", [8, 128, 1024])
            W["e_gate"] = self.din("e_gate", [8, 128, 1024])
            W["e_xbc"] = self.din("e_xbc", [12, 128, 1024])
            W["e_z"] = self.din("e_z", [2, 128, 4096])
            W["e_dt"] = self.din("e_dt", [128, 128])
            W["e_out"] = self.din("e_out", [8, 128, 2048])
            W["prm0"] = self.din("prm0", [128, NPRM0])
        if L1:
            W["o_qk"] = self.din("o_qk", [8, 128, 1024])
            W["o_vg"] = self.din("o_vg", [4, 128, 4096])
            W["o_gl"] = self.din("o_gl", [128, 128])
            W["o_g2"] = self.din("o_g2", [16, 512])
            W["o_out"] = self.din("o_out", [8, 128, 1024])
            W["prm1"] = self.din("prm1", [128, NPRM1])
        for L in self.layers:
            W[f"xq{L}"] = self.din(f"xq{L}", [8, 128, 1024])
            W[f"xk{L}"] = self.din(f"xk{L}", [8, 128, 1024])
            W[f"xv{L}"] = self.din(f"xv{L}", [2, 128, 4096])
            W[f"xo{L}"] = self.din(f"xo{L}", [8, 128, 1024])
            W[f"w1{L}"] = self.din(f"w1{L}", [32, 128, 1024])
            W[f"w2{L}"] = self.din(f"w2{L}", [8, 128, 4096])
        self.W = W
        self.banks = [S.psum(f"bank{i}", [128, 512], F32) for i in range(8)]
        S.banks = self.banks[0:6]
        self.PL = self.banks[6:8]
        self.slots = [S.sbuf(f"slot{i}", [128, 4096], BF16) for i in range(2 if self.debug else 4)]
        self.slot_i = 0
        self.H32 = S.sbuf("H32", [128, 8, 512], F32, nparts=8)
        P = Arena(S, "P", 17536)
        Tm = Arena(S, "Tm", 18688)
        self.P, self.Tm = P, Tm
        self.HT = P.alloc([8, 512], BF16)
        self.identF = P.alloc([128], F32)
        self.identB = P.alloc([128], BF16)
        self.tri = P.alloc([128], F32)
        self.onesF = P.alloc([128], F32)
        self.onesB = P.alloc([128], BF16)
        self.negmask = P.alloc([8, 128], BF16)
        self.flag = P.alloc([1], F32)
        self.EPSC = P.alloc([1], F32)
        kt, vv = P.alloc([8, 256], BF16), P.alloc([2, 1024], BF16)
        self.KT = {L: kt for L in self.layers}
        self.VV = {L: vv for L in self.layers}
        self.HT2 = P.alloc([8, 512], BF16)
        self.HTs = [self.HT, self.HT2]
        self.ch_ht = [Buf("ch_ht0", None), Buf("ch_ht1", None)]
        self.ch_gs = Buf("ch_gs", None)
        self.RB = [P.alloc([512], BF16) for _ in range(2)]
        self.RSQ = [P.alloc([512], BF16) for _ in range(2)]
        self.M32 = P.alloc([512], F32)
        self.RSTD = P.alloc([512], F32)
        self.NMR = P.alloc([512], F32)
        self.ln_i = 0
        if L0:
            self.prm0 = P.alloc([NPRM0], F32)
            self.wdt = P.alloc([8, 16], BF16)
            self.HS32 = P.alloc([1024], F32)
            self.HBF = P.alloc([1024], BF16)
            self.UH = P.alloc([8, 30], BF16)
            self.XRH = P.alloc([12, 3], BF16)
            G = Arena(S, "G", 4608)
            self.DG = [G.alloc([31, 128], BF16) for _ in range(2)]
            self.DGX = [G.alloc([4, 128], BF16) for _ in range(2)]
            self.AROW = P.alloc([16], F32)
            self.DTV = P.alloc([4, 16], F32)
            self.ADT = P.alloc([4, 16], F32)
            self.SM = [P.alloc([8, 16], F32) for _ in range(2)]
            self.SS = P.alloc([4], F32)
        if L1:
            self.prm1 = P.alloc([NPRM1], F32)
            self.wgl = P.alloc([8, 16], BF16)
            self.wg2 = P.alloc([512], BF16)
            self.GS32 = P.alloc([1024], F32)
            self.GSBF = P.alloc([1024], BF16)
            self.rmask = P.alloc([512], F32)
            self.SS1 = P.alloc([8], F32)
        self.setup()
        self.kv_setup(self.layers[0])
        fused = len(self.layers) == 2
        seq = [("pre", self.xpT, t, None) for t in range(NP)] + [("main0", self.xT, t, None) for t in range(NT)]
        hmid = None
        if fused:
            hmid = S.dram("hmid", [D, NTOK], F32)
            hmid.nparts = NT
            hmid.last_w = [None] * NT
            hmid.readers = [[] for _ in range(NT)]
            seq += [("main1", hmid, t, t) for t in range(NT)]
        self.prefetch_ht(seq[0][1], seq[0][2], 0, seq[0][3])
        for i, (kind, src, t, part) in enumerate(seq):
            self.HT = self.HTs[i % 2]
            nxt = seq[i + 1] if i + 1 < len(seq) else None

            def pf():
                if nxt is not None and not (nxt[0] == "main1" and kind != "main1"):
                    self.prefetch_ht(nxt[1], nxt[2], (i + 1) % 2, nxt[3])

            L_first = self.layers[0]
            if kind == "pre":
                if L_first == 0:
                    self.mixer0(state_only=True, halo=(t == NP - 1))
                else:
                    self.mixer1(state_only=True)
                pf()
                if t == NP - 1:
                    self.apply_flag(L_first)
            elif kind == "main0":
                self.load_h32(src, t)
                self.layer(L_first, after_mixer=pf)
                if fused:
                    self.mixer1(state_only=True)
                    self.store(t, hmid, part=t)
                    if t == NT - 1:
                        cc_in = Buf("cc_in", nc.dram_tensor("cc_in", [128, 1024], F32).ap(), 1)
                        cc_out = Buf("cc_out", nc.dram_tensor("cc_out", [256, 1024], F32).ap(), 1)
                        cc_out.sem_inc = 1
                        GS = self.GS32
                        S.op("sp", lambda e: e.dma_start(out=cc_in.t, in_=GS.ap), reads=[GS.a()], writes=[cc_in.a()], dma_buf=cc_in)
                        S.op("pool", lambda e: e.collective_compute("AllGather", ALU.bypass, replica_groups=[[0, 1], [2, 3], [4, 5], [6, 7]],
                                                                    ins=[cc_in.t.opt()], outs=[cc_out.t.opt()]),
                             reads=[cc_in.a()], writes=[cc_out.a()], dma_buf=cc_out)
                        S.op("sp", lambda e: e.dma_start(out=GS.ap, in_=cc_out.t[0:128, :]), reads=[cc_out.a()], writes=[GS.a()],
                             dma_buf=self.ch_gs)
                        self.apply_flag(1)
                        self.kv_setup(1)
                        self.prefetch_ht(hmid, 0, (i + 1) % 2, 0)
                else:
                    self.store(t, self.outT)
            else:
                self.load_h32(src, t, part=part)
                self.layer(1, after_mixer=pf)
                self.store(t, self.outT)
        S.emit(final_wait_bufs=[self.H32] + [v[1] for v in self.dbg_out.values()])

    def setup(self):
        S = self.S
        P = self.P
        Tm_buf = self.Tm.buf
        S.op("pool", lambda e: e.memset(self.EPSC.ap, EPS), writes=[self.EPSC.a()])

        def aff(view, pattern, cm, op, fill, inval):
            S.op("pool", lambda e: e.memset(view.ap, inval), writes=[view.a()])
            S.op("pool", lambda e: e.affine_select(out=view.ap, in_=view.ap, pattern=pattern, compare_op=op,
                                                   fill=fill, base=0, channel_multiplier=cm),
                 reads=[view.a()], writes=[view.a()])

        aff(self.identF, [[-1, 128]], 1, ALU.not_equal, 1.0, 0.0)
        aff(self.identB, [[-1, 128]], 1, ALU.not_equal, 1.0, 0.0)
        aff(self.tri, [[1, 128]], -1, ALU.is_ge, 0.0, 1.0)
        aff(self.negmask, [[0, 8], [1, 128]], -1, ALU.is_ge, -30000.0, 0.0)
        S.op("pool", lambda e: e.memset(self.onesF.ap, 1.0), writes=[self.onesF.a()])
        S.op("pool", lambda e: e.memset(self.onesB.ap, 1.0), writes=[self.onesB.a()])
        S.op("sp", lambda e: e.dma_start(out=self.flag.ap, in_=self.flag_d.t), writes=[self.flag.a()], dma_buf=P.buf)
        if 0 in self.layers:
            S.op("sp", lambda e: e.dma_start(out=self.prm0.ap, in_=self.W["prm0"].t), writes=[self.prm0.a()], dma_buf=P.buf)
            S.op("pool", lambda e: e.dma_start(out=self.wdt.ap, in_=self.W["e_dt"].t.rearrange("p (k c) -> p k c", k=8)),
                 writes=[self.wdt.a()], dma_buf=P.buf)
            for v in (self.HS32, self.HBF, self.UH, self.XRH):
                S.op("pool", lambda e, v=v: e.memset(v.ap, 0.0), writes=[v.a()])
        if 1 in self.layers:
            S.op("sp", lambda e: e.dma_start(out=self.prm1.ap, in_=self.W["prm1"].t), writes=[self.prm1.a()], dma_buf=P.buf)
            S.op("pool", lambda e: e.dma_start(out=self.wgl.ap, in_=self.W["o_gl"].t.rearrange("p (k c) -> p k c", k=8)),
                 writes=[self.wgl.a()], dma_buf=P.buf)
            S.op("pool", lambda e: e.dma_start(out=self.wg2[0:16, :], in_=self.W["o_g2"].t), writes=[self.wg2.a()], dma_buf=P.buf)
            for v in (self.GS32, self.GSBF):
                S.op("pool", lambda e, v=v: e.memset(v.ap, 0.0), writes=[v.a()])
            S.op("pool", lambda e: e.memset(self.rmask.ap, 1.0), writes=[self.rmask.a()])
            S.op("pool", lambda e: e.memset(self.rmask[:, 0:512:128], 0.0), writes=[self.rmask.a()])
        if 0 in self.layers:
            p0 = self.prm0
            self.act(self.AROW.ap, self.AROW.a(), p0[:, self.o_alog:self.o_alog + 16], p0.a(), AF.Exp)
            self.ts(self.AROW.ap, self.AROW.a(), self.AROW.ap, self.AROW.a(), -1.0, None, ALU.mult)

    o_convw, o_convb, o_clng, o_clnb = 0, 248, 256, 264
    o_scw, o_scb, o_dtb, o_alog, o_dsk = 272, 320, 332, 348, 364
    o_normg, o_lng0, o_lnb0 = 380, 1404, 1428
    o_bg, o_hng, o_lng1, o_lnb1 = 0, 4, 260, 284

    def lnp(self, L, i):
        if L == 0:
            p, og, ob = self.prm0, self.o_lng0, self.o_lnb0
        else:
            p, og, ob = self.prm1, self.o_lng1, self.o_lnb1
        return p, og + i * 8, ob + i * 8

    def kv_setup(self, L):
        S = self.S
        self.Tm.top = 0
        self.MEMT = self.Tm.alloc([8, 256], BF16)
        KT, VV, M = self.KT[L], self.VV[L], self.MEMT
        S.op("pool", lambda e: e.dma_start(out=M.ap, in_=self.memT_d.t.rearrange("(j p) m -> p j m", p=128)),
             reads=[self.memT_d.a()], writes=[M.a()], dma_buf=self.Tm.buf)
        wk = self.W[f"xk{L}"]
        for g in range(2):
            w, wacc = self.wload(wk.t[g * 4:(g + 1) * 4].rearrange("g p f -> p g f"), wk, [4, 8, 128])
            for jj in range(4):
                j = g * 4 + jj
                ps = S.ps()
                for kc in range(8):
                    self.mm(ps.t[:, 0:256], ps.a(), w[:, jj, kc, :], wacc, M[:, kc, :], M.a(), kc == 0, kc == 7)
                self.cp(KT[:, j, :], KT.a(j), ps.t[:, 0:256], ps.a())
        wv = self.W[f"xv{L}"]
        for half in range(2):
            w, wacc = self.wload(wv.t[half], wv, [8, 512])
            for mc in range(2):
                ps = S.ps()
                for kc in range(8):
                    self.mm(ps.t[:, :], ps.a(), M[:, kc, mc * 128:(mc + 1) * 128], M.a(), w[:, kc, :], wacc, kc == 0, kc == 7)
                self.cp(VV[:, mc, half * 512:(half + 1) * 512], VV.a(mc), ps.t[:, :], ps.a())

    def layer(self, L, after_mixer=None):
        if "mix" in self.stages:
            if L == 0:
                self.mixer0()
            else:
                self.mixer1()
        if after_mixer is not None:
            after_mixer()
        if "xa" in self.stages:
            self.xattn(L)
        if "mlp" in self.stages:
            self.mlp(L)

    def prefetch_ht(self, src, t, par, part=None):
        HT = self.HTs[par]
        self.S.op("pool", lambda e: e.dma_start(out=HT.ap, in_=src.t[:, t * T:(t + 1) * T].rearrange("(j p) n -> p j n", p=128)),
                  reads=[src.a(part)], writes=[HT.a()], dma_buf=self.ch_ht[par])

    def load_h32(self, src, t, part=None):
        S, H32 = self.S, self.H32
        for j in range(8):
            S.op("sp", lambda e, j=j: e.dma_start(out=H32.t[:, j, :], in_=src.t[j * 128:(j + 1) * 128, t * T:(t + 1) * T]),
                 reads=[src.a(part)], writes=[H32.a(j)], dma_buf=H32)

    def store(self, t, dst, part=None):
        S, H32 = self.S, self.H32
        for j in range(8):
            S.op("sp", lambda e, j=j: e.dma_start(out=dst.t[j * 128:(j + 1) * 128, t * T:(t + 1) * T], in_=H32.t[:, j, :]),
                 reads=[H32.a(j)], writes=[dst.a(part)], dma_buf=H32)

    def apply_flag(self, L):
        st, sb = (self.HS32, self.HBF) if L == 0 else (self.GS32, self.GSBF)
        self.ts(st.ap, st.a(), st.ap, st.a(), self.flag[:, 0:1], None, ALU.mult, extra_reads=[self.flag.a()])
        self.cp(sb.ap, sb.a(), st.ap, st.a())

    def proj_fm(self, w, wacc, jj, KC, rhs_fn, ps):
        for kc in range(KC):
            r, racc = rhs_fn(kc)
            self.mm(ps.t[:, :], ps.a(), w[:, jj, kc, :], wacc, r, racc, kc == 0, kc == KC - 1)

    def ln_stats(self, chunk_fn, n):
        S = self.S
        p0, p1 = self.PL
        for j in range(n):
            r, racc = chunk_fn(j)
            rb, rs = self.RB[self.ln_i % 2], self.RSQ[self.ln_i % 2]
            self.ln_i += 1
            self.cp(rb.ap, rb.a(), r, racc)
            self.act(rs.ap, rs.a(), r, racc, AF.Square)
            self.mm(p0.t[:, :], p0.a(), self.onesB.ap, self.onesB.a(), rb.ap, rb.a(), j == 0, j == n - 1)
            self.mm(p1.t[:, :], p1.a(), self.onesB.ap, self.onesB.a(), rs.ap, rs.a(), j == 0, j == n - 1)
        M, R, N = self.M32, self.RSTD, self.NMR
        inv = 1.0 / (n * 128)
        S.op("act", lambda e: e.mul(M.ap, p0.t[:, :], inv), reads=[p0.a()], writes=[M.a()])
        self.act(N.ap, N.a(), p0.t[:, :], p0.a(), AF.Square, scale=inv)
        self.stt(R.ap, R.a(), p1.t[:, :], p1.a(), inv, N.ap, N.a(), ALU.mult, ALU.subtract)
        self.act(R.ap, R.a(), R.ap, R.a(), AF.Ln, bias=self.EPSC[:, 0:1], extra_reads=[self.EPSC.a()])
        self.act(R.ap, R.a(), R.ap, R.a(), AF.Exp, scale=-0.5)
        self.stt(N.ap, N.a(), M.ap, M.a(), -1.0, R.ap, R.a(), ALU.mult, ALU.mult)

    def ln_resid(self, L, i):
        H32, HT = self.H32, self.HT
        p, og, ob = self.lnp(L, i)
        self.ln_stats(lambda j: (H32.t[:, j, :], H32.a(j)), 8)
        for j in range(8):
            h = H32.t[:, j, :]
            eng = "dve"
            self.tt(h, H32.a(j), h, H32.a(j), self.RSTD.ap, self.RSTD.a(), ALU.mult, eng=eng)
            self.tt(h, H32.a(j), h, H32.a(j), self.NMR.ap, self.NMR.a(), ALU.add, eng=eng)
            self.act(HT[:, j, :], HT.a(j), h, H32.a(j), AF.Identity, bias=p[:, ob + j:ob + j + 1], scale=p[:, og + j:og + j + 1],
                     extra_reads=[p.a()])
        for j in range(8):
            h = H32.t[:, j, :]
            self.act(h, H32.a(j), h, H32.a(j), AF.Identity, bias=p[:, ob + j:ob + j + 1], scale=p[:, og + j:og + j + 1],
                     extra_reads=[p.a()])

    def out_proj_resid(self, wname, KC, rhs_fn, per_slot):
        S, H32 = self.S, self.H32
        wd = self.W[wname]
        w = wacc = None
        for jo in range(8):
            if jo % per_slot == 0:
                w, wacc = self.wload(wd.t[jo:jo + per_slot].rearrange("g p f -> p g f"), wd, [per_slot, KC, 128])
            ps = S.ps()
            self.proj_fm(w, wacc, jo % per_slot, KC, rhs_fn, ps)
            h = H32.t[:, jo, :]
            self.stt(h, H32.a(jo), h, H32.a(jo), ALPHA, ps.t[:, :], ps.a(), ALU.mult, ALU.add)

    def mixer0(self, state_only=False, halo=True):
        S, Tm, HT = self.S, self.Tm, self.HT
        p0 = self.prm0
        Tm.top = 0
        YC = Tm.alloc([8, 512], F32)
        XS = Tm.alloc([4, 1024], F32, at=YC.off)
        UF = Tm.alloc([8, 512], BF16)
        BT = Tm.alloc([2, 512], BF16)
        CT = Tm.alloc([2, 512], BF16)
        BTOK = Tm.alloc([4, 2, 128], BF16)
        SZ = Tm.alloc([4, 1024], BF16)
        YT = Tm.alloc([8, 512], BF16)
        tmp0 = Tm.top
        SG = [Tm.alloc([512], F32) for _ in range(2)]
        XST = [Tm.alloc([512], F32) for _ in range(2)]
        UB = [Tm.alloc([544], BF16) for _ in range(2)]
        XRB = [Tm.alloc([516], BF16) for _ in range(2)]
        Tm.top = tmp0
        DIAG = Tm.alloc([8, 128], F32)
        LEXP = Tm.alloc([8, 128], F32)
        MT = Tm.alloc([8, 128], BF16)
        XDT = Tm.alloc([16, 64], BF16)
        XWs = [Tm.alloc([16, 64], BF16) for _ in range(2)]
        Y32 = Tm.alloc([1024], F32)
        TMP = Tm.alloc([1024], F32)
        YN = Tm.alloc([1024], BF16)
        hrhs = lambda kc: (HT[:, kc, :], HT.a(kc))
        UH, XRH = self.UH, self.XRH

        do_conv = (not state_only) or halo
        if do_conv:
            wts = {}

            def conv_front(j):
                if j % 4 == 0:
                    wts["v"] = self.wload(self.W["e_val"].t[j:j + 4].rearrange("g p f -> p g f"), self.W["e_val"], [4, 8, 128])
                    wts["g"] = self.wload(self.W["e_gate"].t[j:j + 4].rearrange("g p f -> p g f"), self.W["e_gate"], [4, 8, 128])
                pv, pg = S.ps(), S.ps()
                self.proj_fm(wts["v"][0], wts["v"][1], j % 4, 8, hrhs, pv)
                self.proj_fm(wts["g"][0], wts["g"][1], j % 4, 8, hrhs, pg)
                if not state_only:
                    dg = self.DG[j % 2]
                    cw = self.o_convw + j * 31
                    self.tt(dg.ap, dg.a(), self.identB.ap.unsqueeze(1).to_broadcast([128, 31, 128]), self.identB.a(),
                            p0[:, cw:cw + 31].unsqueeze(2).to_broadcast([128, 31, 128]), p0.a(), ALU.mult)
                return pv, pg

            def conv_mid(j, pv, pg):
                U = UB[j % 2]
                sg = SG[j % 2]
                self.act(sg.ap, sg.a(), pg.t[:, :], pg.a(), AF.Sigmoid)
                self.cp(U[:, 0:30], U.a(), UH[:, j, :], UH.a(), eng="dve")
                self.tt(U[:, 30:542], U.a(), pv.t[:, :], pv.a(), sg.ap, sg.a(), ALU.mult)
                self.cp(UH[:, j, :], UH.a(), U[:, 512:542], U.a(), eng="dve")

            def conv_back(j):
                U = UB[j % 2]
                dg = self.DG[j % 2]
                pc = S.ps()
                for k in range(31):
                    self.mm(pc.t[:, :], pc.a(), dg[:, k, :], dg.a(), U[:, k:k + 512], U.a(), k == 0, k == 30)
                self.act(YC[:, j, :], YC.a(j), pc.t[:, :], pc.a(), AF.Identity,
                         bias=p0[:, self.o_convb + j:self.o_convb + j + 1], extra_reads=[p0.a()])

            pend = conv_front(0)
            for j in range(8):
                conv_mid(j, *pend)
                if j + 1 < 8:
                    pend = conv_front(j + 1)
                if not state_only:
                    conv_back(j)
            if not state_only:
                self.ln_stats(lambda j: (YC[:, j, :], YC.a(j)), 8)
                for j in range(8):
                    yc = YC[:, j, :]
                    self.tt(yc, YC.a(j), yc, YC.a(j), self.RSTD.ap, self.RSTD.a(), ALU.mult)
                    self.tt(yc, YC.a(j), yc, YC.a(j), self.NMR.ap, self.NMR.a(), ALU.add)
                    self.act(UF[:, j, :], UF.a(j), yc, YC.a(j), AF.Silu,
                             bias=p0[:, self.o_clnb + j:self.o_clnb + j + 1], scale=p0[:, self.o_clng + j:self.o_clng + j + 1],
                             extra_reads=[p0.a()])

        wxs = {}

        def xbc_front(j):
            if j % 4 == 0:
                wxs["w"] = self.wload(self.W["e_xbc"].t[j:j + 4].rearrange("g p f -> p g f"), self.W["e_xbc"], [4, 8, 128])
            XR = XRB[j % 2]
            px = S.ps()
            self.proj_fm(wxs["w"][0], wxs["w"][1], j % 4, 8, hrhs, px)
            self.cp(XR[:, 0:3], XR.a(), XRH[:, j, :], XRH.a(), eng="dve")
            self.cp(XR[:, 3:515], XR.a(), px.t[:, :], px.a())
            self.cp(XRH[:, j, :], XRH.a(), XR[:, 512:515], XR.a(), eng="dve")
            dgx = self.DGX[j % 2]
            sw = self.o_scw + j * 4
            self.tt(dgx.ap, dgx.a(), self.identB.ap.unsqueeze(1).to_broadcast([128, 4, 128]), self.identB.a(),
                    p0[:, sw:sw + 4].unsqueeze(2).to_broadcast([128, 4, 128]), p0.a(), ALU.mult)

        def xbc_mid(j):
            XR = XRB[j % 2]
            dgx = self.DGX[j % 2]
            pcx = S.ps()
            for k in range(4):
                self.mm(pcx.t[:, :], pcx.a(), dgx[:, k, :], dgx.a(), XR[:, k:k + 512], XR.a(), k == 0, k == 3)
            scb = p0[:, self.o_scb + j:self.o_scb + j + 1]
            if j < 8:
                X = XST[j % 2]
                self.act(X.ap, X.a(), pcx.t[:, :], pcx.a(), AF.Silu, bias=scb, extra_reads=[p0.a()])
            elif j < 10:
                g = j - 8
                self.act(BT[:, g, :], BT.a(g), pcx.t[:, :], pcx.a(), AF.Silu, bias=scb, extra_reads=[p0.a()])
            else:
                g = j - 10
                self.act(CT[:, g, :], CT.a(g), pcx.t[:, :], pcx.a(), AF.Silu, bias=scb, extra_reads=[p0.a()])

        def xbc_back(j):
            if j < 8:
                X = XST[j % 2]
                pt = S.ps()
                for c in range(4):
                    self.tr(pt.t[:, c * 128:(c + 1) * 128], pt.a(), X[:, c * 128:(c + 1) * 128], X.a(),
                            (self.identF.ap, self.identF.a()))
                self.cp(XS[:, :, j * 128:(j + 1) * 128], XS.a(), pt.t[:, :].rearrange("p (c n) -> p c n", c=4), pt.a())
            elif j < 10:
                g = j - 8
                pt = S.ps()
                ptb = pt.t[:, :].bitcast(BF16)
                for c in range(4):
                    self.tr(ptb[:, c * 128:(c + 1) * 128], pt.a(), BT[:, g, c * 128:(c + 1) * 128], BT.a(g),
                            (self.identB.ap, self.identB.a()))
                self.cp(BTOK[:, :, g, :], BTOK.a(), ptb[:, 0:512].rearrange("p (c n) -> p c n", c=4), pt.a())

        nx = 10 if (state_only and not halo) else 12
        for step in range(nx + 2):
            if step < nx:
                xbc_front(step)
            if 0 <= step - 1 < nx:
                xbc_mid(step - 1)
            if 0 <= step - 2 < nx:
                xbc_back(step - 2)

        wzs = []
        if not state_only:
            wzs = [self.wload(self.W["e_z"].t[half], self.W["e_z"], [8, 512]) for half in range(2)]

        def z_proj(c):
            for half in range(2):
                wz, wza = wzs[half]
                pz = S.ps()
                for kc in range(8):
                    self.mm(pz.t[:, :], pz.a(), HT[:, kc, c * 128:(c + 1) * 128], HT.a(kc), wz[:, kc, :], wza, kc == 0, kc == 7)
                self.act(SZ[:, c, half * 512:(half + 1) * 512], SZ.a(c), pz.t[:, :], pz.a(), AF.Silu)
        DTV, ADT, SM = self.DTV, self.ADT, self.SM
        pdt = S.ps()
        for c in range(4):
            for kc in range(8):
                self.mm(pdt.t[:, c * 16:(c + 1) * 16], pdt.a(), HT[:, kc, c * 128:(c + 1) * 128], HT.a(kc),
                        self.wdt[:, kc, :], self.wdt.a(), kc == 0, kc == 7)
        dtb = p0[:, self.o_dtb:self.o_dtb + 16].unsqueeze(1).to_broadcast([128, 4, 16])
        self.tt(DTV.ap, DTV.a(), pdt.t[:, 0:64].rearrange("p (c h) -> p c h", c=4), pdt.a(), dtb, p0.a(), ALU.add)
        self.act(DTV.ap, DTV.a(), DTV.ap, DTV.a(), AF.Exp)
        self.ts(DTV.ap, DTV.a(), DTV.ap, DTV.a(), 1.0, None, ALU.add)
        self.act(DTV.ap, DTV.a(), DTV.ap, DTV.a(), AF.Ln)
        self.tt(ADT.ap, ADT.a(), DTV.ap, DTV.a(), self.AROW.ap.unsqueeze(1).to_broadcast([128, 4, 16]), self.AROW.a(), ALU.mult)

        HS, HBF = self.HS32, self.HBF

        def ssd_pre(c):
            SMc = self.SM[c % 2]
            ACS, NACS, EACS, TOT, DTE, CD, W2 = (SMc[:, i, :] for i in range(7))
            sma = SMc.a()
            pa = S.ps()
            self.mm(pa.t[:, 0:16], pa.a(), self.tri.ap, self.tri.a(), ADT[:, c, :], ADT.a())
            self.mm(pa.t[:, 16:32], pa.a(), self.onesF.ap, self.onesF.a(), ADT[:, c, :], ADT.a())
            self.cp(ACS, sma, pa.t[:, 0:16], pa.a(), eng="dve")
            self.cp(TOT, sma, pa.t[:, 16:32], pa.a(), eng="dve")
            self.ts(NACS, sma, ACS, sma, -1.0, None, ALU.mult)
            self.act(EACS, sma, ACS, sma, AF.Exp)
            self.tt(DTE, sma, TOT, sma, ACS, sma, ALU.subtract)
            self.act(DTE, sma, DTE, sma, AF.Exp)
            self.act(CD, sma, TOT, sma, AF.Exp)
            self.tt(W2, sma, DTV[:, c, :], DTV.a(), DTE, sma, ALU.mult)
            xs3 = XS[:, c, :].rearrange("p (h d) -> p h d", h=16)
            XWc = XWs[c % 2]
            self.tt(XWc.ap, XWc.a(), xs3, XS.a(c), W2.unsqueeze(2).to_broadcast([128, 16, 64]), sma, ALU.mult)

        ssd_pre(0)
        for c in range(4):
            SMc = self.SM[c % 2]
            ACS, NACS, EACS, TOT, DTE, CD, W2 = (SMc[:, i, :] for i in range(7))
            sma = SMc.a()
            XW = XWs[c % 2]
            xs3 = XS[:, c, :].rearrange("p (h d) -> p h d", h=16)
            if state_only and c + 1 < 4:
                ssd_pre(c + 1)
            if not state_only:
                z_proj(c)
                self.tt(XDT.ap, XDT.a(), xs3, XS.a(c), DTV[:, c, :].unsqueeze(2).to_broadcast([128, 16, 64]), DTV.a(), ALU.mult)
                for g in range(2):
                    pcb = S.ps()
                    self.mm(pcb.t[:, 0:128], pcb.a(), BT[:, g, c * 128:(c + 1) * 128], BT.a(g),
                            CT[:, g, c * 128:(c + 1) * 128], CT.a(g))
                    self.tt(DIAG.ap, DIAG.a(), self.identF.ap.unsqueeze(1).to_broadcast([128, 8, 128]), self.identF.a(),
                            ACS[:, g * 8:(g + 1) * 8].unsqueeze(2).to_broadcast([128, 8, 128]), sma, ALU.mult)
                    for half in range(2):
                        pd = S.ps()
                        self.mm(pd.t[:, :], pd.a(), self.onesF.ap, self.onesF.a(),
                                DIAG[:, half * 4:(half + 1) * 4, :].rearrange("p a b -> p (a b)"), DIAG.a(), True, False)
                        self.mm(pd.t[:, :], pd.a(), self.identB.ap, self.identB.a(),
                                self.negmask[:, half * 4:(half + 1) * 4, :].rearrange("p a b -> p (a b)"), self.negmask.a(), False, True)
                        for q in range(4):
                            hh = half * 4 + q
                            h = g * 8 + hh
                            self.act(LEXP[:, hh, :], LEXP.a(hh), pd.t[:, q * 128:(q + 1) * 128], pd.a(), AF.Exp,
                                     bias=NACS[:, h:h + 1], extra_reads=[sma])
                    self.tt(MT.ap, MT.a(), LEXP.ap, LEXP.a(),
                            pcb.t[:, 0:128].unsqueeze(1).to_broadcast([128, 8, 128]), pcb.a(), ALU.mult)
                    pyg = S.ps()
                    for hh in range(8):
                        h = g * 8 + hh
                        self.mm(pyg.t[:, hh * 64:(hh + 1) * 64], pyg.a(), MT[:, hh, :], MT.a(), XDT[:, h, :], XDT.a())
                    pog = S.ps()
                    self.mm(pog.t[:, :], pog.a(), CT[:, g, c * 128:(c + 1) * 128], CT.a(g), HBF[:, g * 512:(g + 1) * 512], HBF.a())
                    ysl = Y32[:, g * 512:(g + 1) * 512]
                    self.tt(ysl.rearrange("p (h d) -> p h d", h=8), Y32.a(), pog.t[:, :].rearrange("p (h d) -> p h d", h=8), pog.a(),
                            EACS[:, g * 8:(g + 1) * 8].unsqueeze(2).to_broadcast([128, 8, 64]), sma, ALU.mult)
                    self.tt(ysl, Y32.a(), ysl, Y32.a(), pyg.t[:, :], pyg.a(), ALU.add)
                if c + 1 < 4:
                    ssd_pre(c + 1)
                dsk = p0[:, self.o_dsk:self.o_dsk + 16].unsqueeze(2).to_broadcast([128, 16, 64])
                self.tt(TMP.ap.rearrange("p (h d) -> p h d", h=16), TMP.a(), xs3, XS.a(c), dsk, p0.a(), ALU.mult)
                self.tt(Y32.ap, Y32.a(), Y32.ap, Y32.a(), TMP.ap, TMP.a(), ALU.add)
            if not state_only and c == 0:
                self.dbg("xs", XS[:, 0, :], XS.a(0), [1024])
                self.dbg("dtv", DTV[:, 0, :], DTV.a(), [16])
                self.dbg("acs", ACS, sma, [16])
                self.dbg("y32", Y32.ap, Y32.a(), [1024])
                self.dbg("sz", SZ[:, 0, :], SZ.a(0), [1024])
                self.dbg("mt", MT.ap, MT.a(), [8, 128])
                self.dbg("lexp", LEXP.ap, LEXP.a(), [8, 128])
            def state_update():
                pst = [S.ps(), S.ps()]
                for g in range(2):
                    self.mm(pst[g].t[:, :], pst[g].a(), BTOK[:, c, g, :], BTOK.a(),
                            XW[:, g * 8:(g + 1) * 8, :].rearrange("p a b -> p (a b)"), XW.a())
                for g in range(2):
                    hs = HS[:, g * 512:(g + 1) * 512]
                    self.tt(hs.rearrange("p (h d) -> p h d", h=8), HS.a(), hs.rearrange("p (h d) -> p h d", h=8), HS.a(),
                            CD[:, g * 8:(g + 1) * 8].unsqueeze(2).to_broadcast([128, 8, 64]), sma, ALU.mult)
                    self.tt(hs, HS.a(), hs, HS.a(), pst[g].t[:, :], pst[g].a(), ALU.add)
                self.cp(HBF.ap, HBF.a(), HS.ap, HS.a())

            if state_only:
                state_update()
                continue
            self.tt(Y32.ap, Y32.a(), Y32.ap, Y32.a(), SZ[:, c, :], SZ.a(c), ALU.mult)
            SSv = self.SS
            self.tt(TMP.ap, TMP.a(), Y32.ap, Y32.a(), Y32.ap, Y32.a(), ALU.mult)
            S.op("dve", lambda e: e.tensor_reduce(out=SSv[:, 0:2], in_=TMP.ap.rearrange("p (g n) -> p g n", g=2),
                                                  axis=mybir.AxisListType.X, op=ALU.add), reads=[TMP.a()], writes=[SSv.a()])
            self.ts(SSv[:, 0:2], SSv.a(), SSv[:, 0:2], SSv.a(), 1.0 / 512, EPS, ALU.mult, ALU.add)
            self.act(SSv[:, 0:2], SSv.a(), SSv[:, 0:2], SSv.a(), AF.Sqrt)
            S.op("dve", lambda e: e.reciprocal(out=SSv[:, 2:4], in_=SSv[:, 0:2]), reads=[SSv.a()], writes=[SSv.a()])
            for g in range(2):
                ysl = Y32[:, g * 512:(g + 1) * 512]
                self.stt(YN[:, g * 512:(g + 1) * 512], YN.a(), ysl, Y32.a(), SSv[:, 2 + g:3 + g],
                         p0[:, self.o_normg + g * 512:self.o_normg + (g + 1) * 512], p0.a(), ALU.mult, ALU.mult,
                         extra_reads=[SSv.a()])
            pt = S.ps()
            ptb = pt.t[:, :].bitcast(BF16)
            for j in range(8):
                self.tr(ptb[:, j * 128:(j + 1) * 128], pt.a(), YN[:, j * 128:(j + 1) * 128], YN.a(),
                        (self.identB.ap, self.identB.a()))
            self.cp(YT[:, :, c * 128:(c + 1) * 128], YT.a(), ptb.rearrange("p (j n) -> p j n", j=8), pt.a())
            state_update()
        if state_only:
            return
        mix = lambda kc: (UF[:, kc, :], UF.a(kc)) if kc < 8 else (YT[:, kc - 8, :], YT.a(kc - 8))
        self.out_proj_resid("e_out", 16, mix, 2)
        self.ln_resid(0, 0)

    def xattn(self, L):
        S, Tm, HT = self.S, self.Tm, self.HT
        Tm.top = 0
        QT = Tm.alloc([8, 512], BF16)
        OT = Tm.alloc([8, 512], BF16)
        PT = [Tm.alloc([512], BF16) for _ in range(4)]
        RDEN = Tm.alloc([512], F32)
        KT, VV = self.KT[L], self.VV[L]
        hrhs = lambda kc: (HT[:, kc, :], HT.a(kc))
        wq = self.W[f"xq{L}"]
        for j in range(8):
            if j % 4 == 0:
                w, wa = self.wload(wq.t[j:j + 4].rearrange("g p f -> p g f"), wq, [4, 8, 128])
            pq = S.ps()
            self.proj_fm(w, wa, j % 4, 8, hrhs, pq)
            S.op("act", lambda e, j=j, pq=pq: e.mul(QT[:, j, :], pq.t[:, :], 1.0 / 16.0), reads=[pq.a()], writes=[QT.a(j)])
        def scores(hd):
            pts = [PT[(hd % 2) * 2], PT[(hd % 2) * 2 + 1]]
            for mc in range(2):
                psc = S.ps()
                for dc in range(2):
                    j = 2 * hd + dc
                    self.mm(psc.t[:, :], psc.a(), KT[:, j, mc * 128:(mc + 1) * 128], KT.a(j), QT[:, j, :], QT.a(j), dc == 0, dc == 1)
                self.act(pts[mc].ap, pts[mc].a(), psc.t[:, :], psc.a(), AF.Exp)

        def den_pv(hd):
            pts = [PT[(hd % 2) * 2], PT[(hd % 2) * 2 + 1]]
            pden = S.ps()
            for mc in range(2):
                self.mm(pden.t[:, :], pden.a(), self.onesB.ap, self.onesB.a(), pts[mc].ap, pts[mc].a(), mc == 0, mc == 1)
            self.act(RDEN.ap, RDEN.a(), pden.t[:, :], pden.a(), AF.Ln)
            self.act(RDEN.ap, RDEN.a(), RDEN.ap, RDEN.a(), AF.Exp, scale=-1.0)
            for dc in range(2):
                j = 2 * hd + dc
                pov = S.ps()
                for mc in range(2):
                    self.mm(pov.t[:, :], pov.a(), VV[:, mc, j * 128:(j + 1) * 128], VV.a(mc), pts[mc].ap, pts[mc].a(), mc == 0, mc == 1)
                self.tt(OT[:, j, :], OT.a(j), pov.t[:, :], pov.a(), RDEN.ap, RDEN.a(), ALU.mult)

        scores(0)
        for hd in range(4):
            if hd + 1 < 4:
                scores(hd + 1)
            den_pv(hd)
        self.out_proj_resid(f"xo{L}", 8, lambda kc: (OT[:, kc, :], OT.a(kc)), 4)
        self.ln_resid(L, 1)

    def mlp(self, L):
        S, Tm, HT = self.S, self.Tm, self.HT
        Tm.top = 0
        H1 = Tm.alloc([32, 512], BF16)
        RL = [Tm.alloc([512], F32) for _ in range(4)]
        hrhs = lambda kc: (HT[:, kc, :], HT.a(kc))
        w1 = self.W[f"w1{L}"]
        for jf in range(32):
            if jf % 4 == 0:
                w, wa = self.wload(w1.t[jf:jf + 4].rearrange("g p f -> p g f"), w1, [4, 8, 128])
            pf = S.ps()
            self.proj_fm(w, wa, jf % 4, 8, hrhs, pf)
            r = RL[jf % 4]
            self.act(r.ap, r.a(), pf.t[:, :], pf.a(), AF.Relu)
            self.act(H1[:, jf, :], H1.a(jf), r.ap, r.a(), AF.Square)
        self.out_proj_resid(f"w2{L}", 32, lambda kc: (H1[:, kc, :], H1.a(kc)), 1)
        self.ln_resid(L, 2)

    def mixer1(self, state_only=False):
        S, Tm, HT = self.S, self.Tm, self.HT
        p1 = self.prm1
        Tm.top = 0
        QKT = Tm.alloc([8, 512], BF16)
        QTT = Tm.alloc([4, 512], BF16)
        KTT = Tm.alloc([4, 512], BF16)
        KEN = Tm.alloc([4, 512], BF16)
        KETOK = Tm.alloc([4, 4, 128], BF16)
        VTOK = Tm.alloc([4, 1024], BF16)
        SGO = Tm.alloc([4, 1024], BF16)
        EBL = Tm.alloc([4, 4], F32)
        GLT = Tm.alloc([512], BF16)
        BCS = Tm.alloc([512], F32)
        E1 = Tm.alloc([512], F32)
        OTF = Tm.alloc([8, 512], BF16)
        ATTs = [Tm.alloc([128], BF16) for _ in range(2)]
        O32 = Tm.alloc([1024], F32)
        TMP = Tm.alloc([1024], F32)
        ON = Tm.alloc([1024], BF16)
        LGS = Tm.alloc([4, 512], F32, at=O32.off)
        hrhs = lambda kc: (HT[:, kc, :], HT.a(kc))
        GS, GSB = self.GS32, self.GSBF
        wqk = self.W["o_qk"]
        for j in range(8):
            if j % 4 == 0:
                w, wa = self.wload(wqk.t[j:j + 4].rearrange("g p f -> p g f"), wqk, [4, 8, 128])
            if state_only and j < 4:
                continue
            pq = S.ps()
            self.proj_fm(w, wa, j % 4, 8, hrhs, pq)
            self.cp(QKT[:, j, :], QKT.a(j), pq.t[:, :], pq.a())
        pgl = S.ps()
        for kc in range(8):
            self.mm(pgl.t[0:16, :], pgl.a(), self.wgl[:, kc, :], self.wgl.a(), HT[:, kc, :], HT.a(kc), kc == 0, kc == 7)
        self.cp(GLT[0:16, :], GLT.a(), pgl.t[0:16, :], pgl.a())
        wvg = self.W["o_vg"]

        def vg_proj(half):
            wz, wza = self.wload(wvg.t[half], wvg, [8, 512])
            for c in range(4):
                pz = S.ps()
                for kc in range(8):
                    self.mm(pz.t[:, :], pz.a(), HT[:, kc, c * 128:(c + 1) * 128], HT.a(kc), wz[:, kc, :], wza, kc == 0, kc == 7)
                if half < 2:
                    self.cp(VTOK[:, c, half * 512:(half + 1) * 512], VTOK.a(c), pz.t[:, :], pz.a())
                else:
                    self.act(SGO[:, c, (half - 2) * 512:(half - 1) * 512], SGO.a(c), pz.t[:, :], pz.a(), AF.Silu)

        for h in range(4):
            pg = S.ps()
            self.mm(pg.t[:, :], pg.a(), self.wg2[0:16, h * 128:(h + 1) * 128], self.wg2.a(), GLT[0:16, :], GLT.a())
            self.ts(LGS[:, h, :], LGS.a(h), pg.t[:, :], pg.a(), p1[:, self.o_bg + h:self.o_bg + h + 1], -1.0, ALU.add, ALU.mult,
                    extra_reads=[p1.a()])
        for h in range(4):
            LGh = LGS[:, h, :]
            la = LGS.a(h)
            self.act(LGh, la, LGh, la, AF.Exp)
            self.ts(LGh, la, LGh, la, 1.0, None, ALU.add)
            self.act(LGh, la, LGh, la, AF.Ln)
            self.ts(LGh, la, LGh, la, -1.0 / 16.0, None, ALU.mult)
            S.op("dve", lambda e, LGh=LGh: e.tensor_tensor_scan(out=BCS.ap, data0=self.rmask.ap, data1=LGh, initial=0.0,
                                                                op0=ALU.mult, op1=ALU.add),
                 reads=[self.rmask.a(), la], writes=[BCS.a()])
            if not state_only:
                self.act(E1.ap, E1.a(), BCS.ap, BCS.a(), AF.Exp)
                self.stt(QTT[:, h, :], QTT.a(h), QKT[:, h, :], QKT.a(h), float(128 ** -0.5), E1.ap, E1.a(), ALU.mult, ALU.mult)
                self.act(E1.ap, E1.a(), BCS.ap, BCS.a(), AF.Exp, scale=-1.0)
                self.tt(KTT[:, h, :], KTT.a(h), QKT[:, 4 + h, :], QKT.a(4 + h), E1.ap, E1.a(), ALU.mult)
            bl = BCS[:, 127:512:128]
            self.act(EBL[:, h, :], EBL.a(), bl, BCS.a(), AF.Exp)
            self.tt(E1.ap.rearrange("p (c n) -> p c n", c=4), E1.a(), bl.unsqueeze(2).to_broadcast([128, 4, 128]), BCS.a(),
                    BCS.ap.rearrange("p (c n) -> p c n", c=4), BCS.a(), ALU.subtract)
            self.act(E1.ap, E1.a(), E1.ap, E1.a(), AF.Exp)
            self.tt(KEN[:, h, :], KEN.a(h), QKT[:, 4 + h, :], QKT.a(4 + h), E1.ap, E1.a(), ALU.mult)
            if h < (2 if state_only else 4):
                vg_proj(h)
            pt = S.ps()
            ptb = pt.t[:, :].bitcast(BF16)
            for c in range(4):
                self.tr(ptb[:, c * 128:(c + 1) * 128], pt.a(), KEN[:, h, c * 128:(c + 1) * 128], KEN.a(h),
                        (self.identB.ap, self.identB.a()))
            self.cp(KETOK[:, :, h, :], KETOK.a(), ptb[:, 0:512].rearrange("p (c n) -> p c n", c=4), pt.a())
        for c in range(4):
            if not state_only:
                po = [S.ps(), S.ps()]
                def att_step(h):
                    pat = S.ps()
                    A = ATTs[h % 2]
                    self.mm(pat.t[:, 0:128], pat.a(), KTT[:, h, c * 128:(c + 1) * 128], KTT.a(h),
                            QTT[:, h, c * 128:(c + 1) * 128], QTT.a(h))
                    self.tt(A.ap, A.a(), pat.t[:, 0:128], pat.a(), self.tri.ap, self.tri.a(), ALU.mult)

                att_step(0)
                for h in range(4):
                    if h + 1 < 4:
                        att_step(h + 1)
                    ATT = ATTs[h % 2]
                    osl = po[h // 2].t[:, (h % 2) * 256:(h % 2 + 1) * 256]
                    self.mm(osl, po[h // 2].a(), ATT.ap, ATT.a(), VTOK[:, c, h * 256:(h + 1) * 256], VTOK.a(c), True, False)
                    self.mm(osl, po[h // 2].a(), QTT[:, h, c * 128:(c + 1) * 128], QTT.a(h), GSB[:, h * 256:(h + 1) * 256], GSB.a(),
                            False, True)
            if not state_only:
                for g in range(2):
                    self.cp(O32[:, g * 512:(g + 1) * 512], O32.a(), po[g].t[:, :], po[g].a())
            pst = [S.ps(), S.ps()]
            for h in range(4):
                self.mm(pst[h // 2].t[:, (h % 2) * 256:(h % 2 + 1) * 256], pst[h // 2].a(), KETOK[:, c, h, :], KETOK.a(),
                        VTOK[:, c, h * 256:(h + 1) * 256], VTOK.a(c))
            for h in range(4):
                gs = GS[:, h * 256:(h + 1) * 256]
                self.stt(gs, GS.a(), gs, GS.a(), EBL[:, h, c:c + 1], pst[h // 2].t[:, (h % 2) * 256:(h % 2 + 1) * 256], pst[h // 2].a(),
                         ALU.mult, ALU.add, extra_reads=[EBL.a()])
            self.cp(GSB.ap, GSB.a(), GS.ap, GS.a())
            if state_only:
                continue
            SSv = self.SS1
            self.tt(TMP.ap, TMP.a(), O32.ap, O32.a(), O32.ap, O32.a(), ALU.mult)
            S.op("dve", lambda e: e.tensor_reduce(out=SSv[:, 0:4], in_=TMP.ap.rearrange("p (g n) -> p g n", g=4),
                                                  axis=mybir.AxisListType.X, op=ALU.add), reads=[TMP.a()], writes=[SSv.a()])
            self.ts(SSv[:, 0:4], SSv.a(), SSv[:, 0:4], SSv.a(), 1.0 / 256, EPS, ALU.mult, ALU.add)
            self.act(SSv[:, 0:4], SSv.a(), SSv[:, 0:4], SSv.a(), AF.Sqrt)
            S.op("dve", lambda e: e.reciprocal(out=SSv[:, 4:8], in_=SSv[:, 0:4]), reads=[SSv.a()], writes=[SSv.a()])
            for h in range(4):
                osl = O32[:, h * 256:(h + 1) * 256]
                self.stt(osl, O32.a(), osl, O32.a(), SSv[:, 4 + h:5 + h], p1[:, self.o_hng:self.o_hng + 256], p1.a(),
                         ALU.mult, ALU.mult, extra_reads=[SSv.a()])
            self.tt(ON.ap, ON.a(), O32.ap, O32.a(), SGO[:, c, :], SGO.a(c), ALU.mult)
            pt = S.ps()
            ptb = pt.t[:, :].bitcast(BF16)
            for j in range(8):
                self.tr(ptb[:, j * 128:(j + 1) * 128], pt.a(), ON[:, j * 128:(j + 1) * 128], ON.a(),
                        (self.identB.ap, self.identB.a()))
            self.cp(OTF[:, :, c * 128:(c + 1) * 128], OTF.a(), ptb.rearrange("p (j n) -> p j n", j=8), pt.a())
        if state_only:
            return
        self.out_proj_resid("o_out", 8, lambda kc: (OTF[:, kc, :], OTF.a(kc)), 4)
        self.ln_resid(1, 0)


def prep_weights(inp, layers):
    f = lambda a: np.ascontiguousarray(a, dtype=np.float32)
    W = {}
    if 0 in layers:
        w_in = inp["even_w_in"][0]
        W["e_val"] = tile_lhsT(w_in[:, 0:1024])
        W["e_gate"] = tile_lhsT(w_in[:, 1024:2048])
        W["e_z"] = tile_rhs(w_in[:, 2048:3072])
        W["e_xbc"] = tile_lhsT(w_in[:, 3072:4608])
        W["e_dt"] = f(w_in[:, 4608:4624].reshape(8, 128, 16).transpose(1, 0, 2).reshape(128, 128))
        W["e_out"] = tile_lhsT(inp["even_w_out"][0])
        prm = np.zeros((128, 1460), np.float32)
        prm[:, 0:248] = inp["even_conv_w"][0].T.reshape(8, 128, 31).transpose(1, 0, 2).reshape(128, 248)
        prm[:, 248:256] = chan(inp["even_conv_b"][0])
        prm[:, 256:264] = chan(inp["even_conv_ln_g"][0])
        prm[:, 264:272] = chan(inp["even_conv_ln_b"][0])
        prm[:, 272:320] = inp["even_ssm_conv_w"][0].T.reshape(12, 128, 4).transpose(1, 0, 2).reshape(128, 48)
        prm[:, 320:332] = chan(inp["even_ssm_conv_b"][0])
        prm[:, 332:348] = rows(inp["even_dt_bias"][0])
        prm[:, 348:364] = rows(inp["even_a_log"][0])
        prm[:, 364:380] = rows(inp["even_d_skip"][0])
        prm[:, 380:1404] = rows(inp["even_ssm_norm_g"][0])
        for i in range(3):
            prm[:, 1404 + i * 8:1412 + i * 8] = chan(inp["ln_g"][0, i])
            prm[:, 1428 + i * 8:1436 + i * 8] = chan(inp["ln_b"][0, i])
        W["prm0"] = prm
    if 1 in layers:
        w_in = inp["odd_w_in"][0]
        W["o_qk"] = tile_lhsT(w_in[:, 0:1024])
        W["o_vg"] = tile_rhs(w_in[:, 1024:3072])
        W["o_gl"] = f(w_in[:, 3072:3088].reshape(8, 128, 16).transpose(1, 0, 2).reshape(128, 128))
        W["o_g2"] = f(inp["odd_w_gate2"][0])
        W["o_out"] = tile_lhsT(inp["odd_w_out"][0])
        prm = np.zeros((128, 320), np.float32)
        prm[:, 0:4] = chan(inp["odd_b_gate"][0])
        prm[:, 4:260] = rows(inp["odd_head_norm_g"][0])
        for i in range(3):
            prm[:, 260 + i * 8:268 + i * 8] = chan(inp["ln_g"][1, i])
            prm[:, 284 + i * 8:292 + i * 8] = chan(inp["ln_b"][1, i])
        W["prm1"] = prm
    for L in layers:
        W[f"xq{L}"] = tile_lhsT(inp["xa_w_q"][L])
        W[f"xk{L}"] = tile_lhsT(inp["xa_w_k"][L])
        W[f"xv{L}"] = tile_rhs(inp["xa_w_v"][L])
        W[f"xo{L}"] = tile_lhsT(inp["xa_w_o"][L])
        W[f"w1{L}"] = tile_lhsT(inp["mlp_w1"][L])
        W[f"w2{L}"] = tile_lhsT(inp["mlp_w2"][L])
    return W


_PROGS = {}


STAGES = ("mix", "xa", "mlp")
DEBUG = False
SAME_ENGINE_SYNC = True


def get_prog(layers, NT, NP):
    key = (tuple(layers), NT, NP, STAGES)
    if key not in _PROGS:
        _PROGS[key] = Prog(tuple(layers), NT, NP, same_engine_sync=SAME_ENGINE_SYNC, stages=STAGES)
    return _PROGS[key]


def run_layers(layers, h, mem, inp, W=None):
    B, L, _ = h.shape
    half = L // 2
    NT = half // T
    prog = get_prog(layers, NT, NT)
    if W is None:
        W = prep_weights(inp, layers)
    in_maps = []
    for core in range(2 * B):
        b, s = core // 2, core % 2
        m = dict(W)
        m["xT"] = np.ascontiguousarray(h[b, s * half:(s + 1) * half].T)
        if s == 0:
            m["xpT"] = np.zeros((D, half), np.float32)
        else:
            m["xpT"] = np.ascontiguousarray(h[b, 0:half].T)
        m["memT"] = np.ascontiguousarray(mem[b].T)
        m["flag"] = np.full((128, 1), float(s), np.float32)
        in_maps.append(m)
    res = run_bass_kernel_spmd(prog.nc, in_maps, core_ids=list(range(2 * B)))
    out = np.empty((B, L, D), np.float32)
    for core in range(2 * B):
        b, s = core // 2, core % 2
        out[b, s * half:(s + 1) * half] = res.results[core]["outT"].T
    return out


def kernel(**inputs):
    inp = {k: np.asarray(v) for k, v in inputs.items()}
    h = np.ascontiguousarray(inp["x"], dtype=np.float32)
    mem = np.ascontiguousarray(inp["mem"], dtype=np.float32)
    return run_layers((0, 1), h, mem, inp)
```
